# Optimizing a Trainium2 kernel written in Bass

```python
import math
import jax, jax.numpy as jnp
from jax import lax
import numpy as np

D_MODEL = 2048
BATCH = 4
SEQ = 8192
DEPTH = 1
DEC_BATCH = 2
DEC_SEQ = 8192
PAST_LEN = 128

HEAD_DIM = 64
N_ATTN_HEADS = (3 * D_MODEL // 4) // HEAD_DIM
ATTN_WIDTH = N_ATTN_HEADS * HEAD_DIM
FOURIER_WIDTH = D_MODEL - ATTN_WIDTH
N_FOURIER_GROUPS = 4
FOURIER_GROUP = FOURIER_WIDTH // N_FOURIER_GROUPS
IN_PROJ_WIDTH = 3 * ATTN_WIDTH + FOURIER_WIDTH
DILATION_PATTERNS = ((128, 1), (512, 4), (2048, 16))
ROPE_THETA = 500000.0
ROPE_DIM = HEAD_DIM // 4
D_FF = 5632
CONV_WIDTH = 3
EPS = 1e-6
MASK_VALUE = -1e30

kernel_name = "hybrid_dilated_attn_fnet_convffn_encoder"


def rms_norm(x, g):
    xf = x.astype(jnp.float32)
    var = jnp.mean(xf * xf, axis=-1, keepdims=True)
    return (xf * lax.rsqrt(var + EPS) * g.astype(jnp.float32)).astype(x.dtype)


def rope_tables(seq_len):
    inv_freq = ROPE_THETA ** (-jnp.arange(0, ROPE_DIM, 2, dtype=jnp.float32) / ROPE_DIM)
    ang = jnp.arange(seq_len, dtype=jnp.float32)[:, None] * inv_freq[None, :]
    return jnp.cos(ang), jnp.sin(ang)


def partial_rotary(x, cos, sin):
    half = ROPE_DIM // 2
    x1 = x[..., :half]
    x2 = x[..., half:ROPE_DIM]
    c = cos[None, :, None, :]
    s = sin[None, :, None, :]
    return jnp.concatenate([x1 * c - x2 * s, x2 * c + x1 * s, x[..., ROPE_DIM:]], axis=-1)


def dilated_window_attention(q, k, v, window, dilation):
    B, S, H, Dh = q.shape
    d = dilation
    R = window // (2 * d)
    T = S // d
    G = B * d

    def to_classes(a):
        return a.reshape(B, T, d, H, Dh).transpose(0, 2, 1, 3, 4).reshape(G, T, H, Dh)

    qc, kc, vc = to_classes(q), to_classes(k), to_classes(v)
    nb = -(-T // R)
    Tp = nb * R
    qb = jnp.pad(qc, ((0, 0), (0, Tp - T), (0, 0), (0, 0))).reshape(G, nb, R, H, Dh)

    def key_windows(a):
        ap = jnp.pad(a, ((0, 0), (R, Tp - T + R), (0, 0), (0, 0))).reshape(G, nb + 2, R, H, Dh)
        return jnp.concatenate([ap[:, :-2], ap[:, 1:-1], ap[:, 2:]], axis=2)

    kw, vw = key_windows(kc), key_windows(vc)
    t_q = (jnp.arange(nb) * R)[:, None] + jnp.arange(R)[None, :]
    t_k = (jnp.arange(nb) * R - R)[:, None] + jnp.arange(3 * R)[None, :]
    diff = t_k[:, None, :] - t_q[:, :, None]
    valid = (jnp.abs(diff) <= R) & (t_k[:, None, :] >= 0) & (t_k[:, None, :] < T)

    scale = 1.0 / math.sqrt(Dh)
    s = jnp.einsum('gnqhd,gnkhd->gnhqk', qb, kw) * scale
    s = jnp.where(valid[None, :, None, :, :], s, MASK_VALUE)
    m = jnp.max(s, axis=-1, keepdims=True)
    p = jnp.exp(s - m)
    den = jnp.sum(p, axis=-1, keepdims=True)
    o = jnp.einsum('gnhqk,gnkhd->gnqhd', p / den, vw)
    lse = (m + jnp.log(den))[..., 0]

    o = o.reshape(G, Tp, H, Dh)[:, :T].reshape(B, d, T, H, Dh)
    o = o.transpose(0, 2, 1, 3, 4).reshape(B, S, H, Dh)
    lse = lse.transpose(0, 1, 3, 2).reshape(G, Tp, H)[:, :T].reshape(B, d, T, H)
    lse = lse.transpose(0, 2, 1, 3).reshape(B, S, H)
    return o, lse


def dilated_mixture_attention(q, k, v):
    outs, lses = [], []
    for window, dilation in DILATION_PATTERNS:
        o, l = dilated_window_attention(q, k, v, window, dilation)
        outs.append(o)
        lses.append(l)
    w = jax.nn.softmax(jnp.stack(lses, axis=0), axis=0)
    out = w[0][..., None] * outs[0]
    for i in range(1, len(outs)):
        out = out + w[i][..., None] * outs[i]
    return out


def fourier_mix(f):
    B, S, _ = f.shape
    fg = f.astype(jnp.float32).reshape(B, S, N_FOURIER_GROUPS, FOURIER_GROUP)
    out = jnp.fft.fft2(fg, axes=(1, 3), norm='ortho').real
    return out.reshape(B, S, FOURIER_WIDTH).astype(f.dtype)


def centred_depthwise_conv(u, w, b):
    S = u.shape[1]
    pad = CONV_WIDTH // 2
    up = jnp.pad(u, ((0, 0), (pad, pad), (0, 0)))
    out = up[:, 0:S] * w[0]
    for j in range(1, CONV_WIDTH):
        out = out + up[:, j:j + S] * w[j]
    return out + b


def encoder_layer(x, norm1_g, w_in, attn_out_g, fourier_out_g, w_out,
                  norm2_g, w_up, conv_w, conv_b, w_down):
    B, S, _ = x.shape
    h = rms_norm(x, norm1_g)
    proj = h @ w_in
    q = proj[..., :ATTN_WIDTH]
    k = proj[..., ATTN_WIDTH:2 * ATTN_WIDTH]
    v = proj[..., 2 * ATTN_WIDTH:3 * ATTN_WIDTH]
    f = proj[..., 3 * ATTN_WIDTH:]

    cos, sin = rope_tables(S)
    shp = (B, S, N_ATTN_HEADS, HEAD_DIM)
    q = partial_rotary(q.astype(jnp.float32).reshape(shp), cos, sin)
    k = partial_rotary(k.astype(jnp.float32).reshape(shp), cos, sin)
    v = v.astype(jnp.float32).reshape(shp)
    attn = dilated_mixture_attention(q, k, v).reshape(B, S, ATTN_WIDTH).astype(x.dtype)

    four = fourier_mix(f)
    merged = jnp.concatenate([rms_norm(attn, attn_out_g), rms_norm(four, fourier_out_g)], axis=-1)
    x = x + merged @ w_out

    h2 = rms_norm(x, norm2_g)
    u = centred_depthwise_conv(h2 @ w_up, conv_w, conv_b)
    gate = u[..., :D_FF]
    val = u[..., D_FF:]
    x = x + (jax.nn.silu(gate) * val) @ w_down
    return x


def encoder(x, norm1_g, w_in, attn_out_g, fourier_out_g, w_out,
            norm2_g, w_up, conv_w, conv_b, w_down, final_g):
    for l in range(DEPTH):
        x = encoder_layer(x, norm1_g[l], w_in[l], attn_out_g[l], fourier_out_g[l], w_out[l],
                          norm2_g[l], w_up[l], conv_w[l], conv_b[l], w_down[l])
    return rms_norm(x, final_g)


def setup_inputs(seed: int = 0) -> dict:
    key = jax.random.key(seed)
    ks = jax.random.split(key, 14)
    f32 = jnp.float32
    conv_base = jnp.array([0.25, 0.5, 0.25], dtype=f32)[None, :, None]
    return {
        'x_prompt': jax.random.normal(ks[0], (BATCH, SEQ, D_MODEL), f32),
        'x_sample': jax.random.normal(ks[1], (DEC_BATCH, DEC_SEQ, D_MODEL), f32),
        'norm1_g': 1.0 + 0.02 * jax.random.normal(ks[2], (DEPTH, D_MODEL), f32),
        'w_in': jax.random.normal(ks[3], (DEPTH, D_MODEL, IN_PROJ_WIDTH), f32) * D_MODEL ** -0.5,
        'attn_out_g': 1.0 + 0.02 * jax.random.normal(ks[4], (DEPTH, ATTN_WIDTH), f32),
        'fourier_out_g': 1.0 + 0.02 * jax.random.normal(ks[5], (DEPTH, FOURIER_WIDTH), f32),
        'w_out': jax.random.normal(ks[6], (DEPTH, D_MODEL, D_MODEL), f32) * D_MODEL ** -0.5,
        'norm2_g': 1.0 + 0.02 * jax.random.normal(ks[7], (DEPTH, D_MODEL), f32),
        'w_up': jax.random.normal(ks[8], (DEPTH, D_MODEL, 2 * D_FF), f32) * D_MODEL ** -0.5,
        'conv_w': conv_base + 0.1 * jax.random.normal(ks[9], (DEPTH, CONV_WIDTH, 2 * D_FF), f32),
        'conv_b': 0.01 * jax.random.normal(ks[10], (DEPTH, 2 * D_FF), f32),
        'w_down': jax.random.normal(ks[11], (DEPTH, D_FF, D_MODEL), f32) * D_FF ** -0.5,
        'final_g': 1.0 + 0.02 * jax.random.normal(ks[12], (D_MODEL,), f32),
    }


def reference(x_prompt, x_sample, norm1_g, w_in, attn_out_g, fourier_out_g, w_out,
              norm2_g, w_up, conv_w, conv_b, w_down, final_g):
    y_prompt = encoder(x_prompt, norm1_g, w_in, attn_out_g, fourier_out_g, w_out,
                       norm2_g, w_up, conv_w, conv_b, w_down, final_g)
    y_sample = encoder(x_sample, norm1_g, w_in, attn_out_g, fourier_out_g, w_out,
                       norm2_g, w_up, conv_w, conv_b, w_down, final_g)
    return (y_prompt, y_sample)
```

```python
import math
import numpy as np
import concourse.bass as bass
import concourse.mybir as mybir
from concourse.bass_utils import run_bass_kernel_spmd

F32 = mybir.dt.float32
BF16 = mybir.dt.bfloat16
AF = mybir.ActivationFunctionType
ALU = mybir.AluOpType

S = 8192
D = 2048
DC = D // 128
NH = 24
DH = 64
AW = NH * DH
FW = 512
NQKV = 3 * AW + FW
DFF = 5632
NFF = DFF // 128
EPS = 1e-6
PAD = 1024
SEM_ROLL = 24000
N_CORES = 8


class Buf:
    __slots__ = ("name", "w", "r", "fw", "dsem_w", "dsem_r", "excl")

    def __init__(self, name, excl=False):
        self.name = name
        self.excl = excl
        self.w = {}
        self.r = {}
        self.fw = {}
        self.dsem_w = None
        self.dsem_r = None


class Prog:
    ENG = ("pe", "act", "dve", "pool", "sp")

    def __init__(self, nc):
        self.nc = nc
        self.q = {e: [] for e in self.ENG}
        self.sem_handles = []
        self.cur_sem = {}
        self.seen = {e: {} for e in self.ENG}
        self.n_ops = 0
        self.dsems = []

    def _new_sem(self, name):
        h = self.nc.alloc_semaphore(name)
        self.sem_handles.append(h)
        return len(self.sem_handles) - 1

    def _eng_event(self, eng):
        cur = self.cur_sem.get(eng)
        if cur is None or cur[1] >= SEM_ROLL:
            cur = [self._new_sem(f"pg_{eng}_{len(self.sem_handles)}"), 0]
            self.cur_sem[eng] = cur
        cur[1] += 1
        return (cur[0], cur[1])

    def _collect(self, eng, reads, writes, pwrites):
        deps = {}

        def add(d):
            for s, v in d.items():
                if deps.get(s, 0) < v:
                    deps[s] = v
        for b in reads:
            add(b.w)
            if b.excl:
                add(b.r)
        for b in writes:
            add(b.w)
            add(b.r)
        for b in pwrites:
            add(b.r)
            add(b.fw)
        seen = self.seen[eng]
        waits = []
        for s, v in deps.items():
            if seen.get(s, 0) < v:
                seen[s] = v
                waits.append((s, v))
        return waits

    def _publish(self, ev, reads, writes, pwrites):
        s, v = ev
        for b in reads:
            if b.r.get(s, 0) < v:
                b.r[s] = v
        for b in writes:
            b.w = {s: v}
            b.fw = {s: v}
            b.r = {}
        for b in pwrites:
            if b.w.get(s, 0) < v:
                b.w[s] = v

    def op(self, eng, fn, reads=(), writes=(), pwrites=(), signal=True):
        waits = self._collect(eng, reads, writes, pwrites)
        ev = None
        if signal:
            ev = self._eng_event(eng)
            self._publish(ev, reads, writes, pwrites)
        self.q[eng].append((waits, fn, ev, 1))
        self.n_ops += 1
        return ev

    def dma(self, eng, fn, src, dst, owner="dst", partial=False):
        if owner == "dst":
            if dst.dsem_w is None:
                dst.dsem_w = [self._new_sem(f"dw_{dst.name}"), 0]
                self.dsems.append(dst.dsem_w)
            ds = dst.dsem_w
        else:
            if src.dsem_r is None:
                src.dsem_r = [self._new_sem(f"dr_{src.name}"), 0]
                self.dsems.append(src.dsem_r)
            ds = src.dsem_r
        reads = (src,)
        writes = () if partial else (dst,)
        pwrites = (dst,) if partial else ()
        waits = self._collect(eng, reads, writes, pwrites)
        serial = not (partial and owner == "dst")
        seen = self.seen[eng]
        if serial and ds[1] > 0 and seen.get(ds[0], 0) < ds[1]:
            seen[ds[0]] = ds[1]
            waits.append((ds[0], ds[1]))
        ds[1] += 16
        ev = (ds[0], ds[1])
        self._publish(ev, reads, writes, pwrites)
        self.q[eng].append((waits, fn, ev, 16))
        self.n_ops += 1
        return ev

    def barrier(self):
        evs = [(c[0], c[1]) for c in self.cur_sem.values()] + [(d[0], d[1]) for d in self.dsems if d[1] > 0]
        for eng in self.ENG:
            seen = self.seen[eng]
            waits = []
            for s, v in evs:
                if seen.get(s, 0) < v:
                    seen[s] = v
                    waits.append((s, v))
            if waits:
                self.q[eng].append((waits, None, None, 0))

    def final_wait(self, eng, bufs):
        waits = self._collect(eng, bufs, (), ())
        self.q[eng].append((waits, None, None, 0))

    def emit(self):
        nc = self.nc
        sems = self.sem_handles

        def run(eng_key):
            def body(e):
                for waits, fn, ev, inc in self.q[eng_key]:
                    for s, v in waits:
                        e.wait_ge(sems[s], v)
                    if fn is None:
                        continue
                    ins = fn(e)
                    if ev is not None:
                        ins.then_inc(sems[ev[0]], inc)
            return body

        with nc.Block() as block:
            block.tensor(run("pe"))
            block.scalar(run("act"))
            block.vector(run("dve"))
            block.gpsimd(run("pool"))
            block.sync(run("sp"))


def _rope_tables():
    half = 8
    inv_freq = (np.float32(500000.0) ** (-(np.arange(0, 16, 2, dtype=np.float32) / np.float32(16)))).astype(np.float32)
    ang = (np.arange(S, dtype=np.float32)[:, None] * inv_freq[None, :]).astype(np.float32)
    cos = np.cos(ang.astype(np.float64)).astype(np.float32)
    sin = np.sin(ang.astype(np.float64)).astype(np.float32)
    C = np.ones((128, S), np.float32)
    Sn = np.zeros((128, S), np.float32)
    for p in range(128):
        dim = p % 64
        if dim < 16:
            C[p] = cos[:, dim % half]
            Sn[p] = sin[:, dim % half]
    R = np.zeros((128, 128), np.float32)
    for hh in range(2):
        for j in range(8):
            R[64 * hh + j + 8, 64 * hh + j] = -1.0
            R[64 * hh + j, 64 * hh + j + 8] = 1.0
    return C, Sn, R


def _attn_masks():
    j = np.arange(128)[:, None]
    p = np.arange(128)[None, :]
    u_of_p = p // 32
    i_of_p = p % 32
    A = np.zeros((4, 128, 512), np.float32)
    for w in range(4):
        tq = 4 * i_of_p + u_of_p
        A[w, :, 0:128] = (np.abs(j - 64 - tq) <= 64)
        A[w, :, 128:256] = (np.abs(j + 64 - tq) <= 64)
        ii = np.arange(32)[None, :]
        for u in range(4):
            for k in range(2):
                c0 = 256 + (2 * u + k) * 32
                if k == 0:
                    A[w, :, c0:c0 + 32] = (np.abs(j - 64 - 32 * w - ii) <= 64)
                else:
                    A[w, :, c0:c0 + 32] = (np.abs(j + 64 - 32 * w - ii) <= 64)
    B = np.zeros((4, 128, 256), np.float32)
    los = [0, 0, 8, 16, 24]
    wid = [8, 16, 16, 16, 8]
    offs = [0, 32, 96, 160, 224]
    for r in range(4):
        for m in range(5):
            for u in range(4):
                il = np.arange(wid[m])[None, :]
                ip = los[m] + il
                tokq = 16 * ip + 4 * u + r
                tokk = 128 * m - 64 + j
                c0 = offs[m] + u * wid[m]
                B[r, :, c0:c0 + wid[m]] = (np.abs(tokk - tokq) <= 64)
    return A, B


def _dft_tables():
    c = np.arange(128)
    ang = 2 * np.pi * np.outer(c, c) / 128.0
    CS = np.stack([np.cos(ang), -np.sin(ang)], axis=1).astype(np.float32)
    s1 = np.arange(128)[:, None, None]
    s2 = np.arange(64)[None, :, None]
    k1 = np.arange(128)[None, None, :]
    angG = 2 * np.pi * ((64 * s1 + s2) * k1 % S) / float(S)
    G = np.stack([np.cos(angG), np.sin(angG)], axis=2).astype(np.float32)
    s2v = np.arange(64)[:, None]
    k2 = np.arange(64)[None, :]
    angE = 2 * np.pi * (s2v * k2 % 64) / 64.0
    scale = 1.0 / math.sqrt(S * 128.0)
    E = np.concatenate([np.cos(angE), np.sin(angE)], axis=0).astype(np.float32) * np.float32(scale)
    return CS, G, E


_CONST_CACHE = {}


def _constants():
    if not _CONST_CACHE:
        C, Sn, R = _rope_tables()
        A, B = _attn_masks()
        CS, G, E = _dft_tables()
        _CONST_CACHE.update(dict(
            c_ident=np.eye(128, dtype=np.float32), c_ropeC=C, c_ropeS=Sn, c_rmat=R,
            c_maskA=np.ascontiguousarray(A.transpose(1, 0, 2)),
            c_maskB=np.ascontiguousarray(B.transpose(1, 0, 2)),
            c_dftCS=CS, c_dftG=G, c_dftE=E))
    return _CONST_CACHE


class _Ctx:
    pass


def _mm_group(P, out_ap, b_out, terms, reads, start_first=True):
    n = len(terms)
    for i, (l, r) in enumerate(terms):
        first, last = (i == 0), (i == n - 1)
        P.op("pe",
             lambda e, l=l, r=r, first=first, last=last: e.matmul(out_ap, lhsT=l, rhs=r, start=first, stop=last),
             reads=reads if (first or last) else (),
             writes=[b_out] if first else (),
             pwrites=() if first else ([b_out] if last else ()),
             signal=(first or last))


def build_program(upto="all", debug=False, lim=None):
    nc = bass.Bass("TRN2", target_bir_lowering=False)
    P = Prog(nc)
    K = _Ctx()
    K.nc, K.P, K.debug = nc, P, debug
    K.lim = lim or {}
    dbg_kind = "ExternalOutput" if debug else "Internal"

    def din(name, shape, dt=F32):
        return nc.dram_tensor(name, list(shape), dt, kind="ExternalInput").ap()

    def dscr(name, shape, dt, dbg=False):
        return nc.dram_tensor(name, list(shape), dt, kind=(dbg_kind if dbg else "Internal")).ap()

    K.x = din("x", [S, D])
    K.w_in = din("w_in", [D, NQKV])
    K.w_out = din("w_out", [D, D])
    K.w_up = din("w_up", [D, 2 * DFF])
    K.w_down = din("w_down", [DFF, D])
    K.g1 = din("g1", [128, DC])
    K.gm = din("gm", [128, DC])
    K.g2 = din("g2", [128, DC])
    K.fg = din("fg", [128, D])
    K.cw = din("cw", [128, 2 * NFF, 3])
    K.cb = din("cb", [128, 2 * NFF])
    K.c_ident = din("c_ident", [128, 128])
    K.c_ropeC = din("c_ropeC", [128, S])
    K.c_ropeS = din("c_ropeS", [128, S])
    K.c_rmat = din("c_rmat", [128, 128])
    K.c_maskA = din("c_maskA", [128, 4, 512])
    K.c_maskB = din("c_maskB", [128, 4, 256])
    K.c_dftCS = din("c_dftCS", [128, 2, 128])
    K.c_dftG = din("c_dftG", [128, 64, 2, 128])
    K.c_dftE = din("c_dftE", [128, 64])
    K.y = nc.dram_tensor("y", [S, D], F32, kind="ExternalOutput").ap()

    K.qkvT = dscr("qkvT_s", [NQKV, S], BF16, dbg=True)
    K.FT_s = dscr("FT_s", [FW, S], BF16, dbg=True)
    K.B_s = dscr("B_s", [4, 2, 64, 128, 128], BF16)
    K.attnT_s = dscr("attnT_s", [AW, S], BF16, dbg=True)
    K.x1_s = dscr("x1_s", [S, D], F32, dbg=True)
    K.h2T_s = dscr("h2T_s", [D, S + 2], BF16, dbg=True)
    K.wout_s = dscr("wout_s", [D, D], BF16)
    K.wup_s = dscr("wup_s", [2 * NFF, 128, DC, 128], BF16)
    K.wdown_s = dscr("wdown_s", [DFF, D], BF16)

    K.b_in = Buf("dram_in")
    K.b_y = Buf("y")
    K.b_qkvT = Buf("qkvT_s")
    K.b_FT = Buf("FT_s")
    K.b_Bs = [Buf(f"B_s{g}") for g in range(4)]
    K.b_attnT = Buf("attnT_s")
    K.b_x1 = Buf("x1_s")
    K.b_h2T = Buf("h2T_s")
    K.b_wout = Buf("wout_s")
    K.b_wup = Buf("wup_s")
    K.b_wdown = Buf("wdown_s")

    sb = nc.alloc_sbuf_tensor
    K.ident = sb("ident", [128, 128], BF16)
    K.b_ident = Buf("ident")
    P.dma("pool", lambda e: e.dma_start(out=K.ident[:, :], in_=K.c_ident), K.b_in, K.b_ident)
    K.g1t = sb("g1t", [128, DC], F32)
    K.g2t = sb("g2t", [128, DC], F32)
    K.gmt = sb("gmt", [128, DC], F32)
    K.b_gains = Buf("gains")
    P.dma("sp", lambda e: e.dma_start(out=K.g1t[:, :], in_=K.g1), K.b_in, K.b_gains, partial=True)
    P.dma("sp", lambda e: e.dma_start(out=K.g2t[:, :], in_=K.g2), K.b_in, K.b_gains, partial=True)
    P.dma("sp", lambda e: e.dma_start(out=K.gmt[:, :], in_=K.gm), K.b_in, K.b_gains, partial=True)
    K.epst = sb("epst", [128, 1], F32)
    K.b_eps = Buf("eps")
    P.op("dve", lambda e: e.memset(K.epst[:, :], EPS), writes=[K.b_eps])

    base = nc.sbuf_bytes_remaining
    phases = ["p1", "pf", "p2", "p3a", "p3b"]
    stop = phases.index(upto) if upto in phases else len(phases) - 1
    out_bufs = []
    _phase1(K)
    if stop >= 1:
        _phase_fnet(K)
    if stop >= 2:
        _phase_attn(K)
    if stop >= 3:
        _phase3a(K)
    if stop >= 4:
        _phase3b(K)
    finals = [K.b_y, K.b_qkvT, K.b_FT, K.b_attnT, K.b_x1, K.b_h2T, K.b_wout, K.b_wup, K.b_wdown] + K.b_Bs
    P.final_wait("sp", finals)
    P.emit()
    return nc


class _Arena:
    def __init__(self, K):
        from contextlib import ExitStack
        self.nc = K.nc
        self.st = ExitStack()

    def __enter__(self):
        self.st.__enter__()
        return self

    def __exit__(self, *a):
        return self.st.__exit__(*a)

    def sb(self, name, shape, dt):
        return self.st.enter_context(self.nc.sbuf_tensor(name, list(shape), dt))

    def ps(self, name, shape, dt=F32):
        return self.st.enter_context(self.nc.psum_tensor(name, list(shape), dt))


def _interleave(main, side, ratio):
    side_live = side is not None
    n = 0
    for _ in main:
        n += 1
        if side_live and n % ratio == 0:
            try:
                next(side)
            except StopIteration:
                side_live = False
    if side_live:
        for _ in side:
            pass


def _rstd_chain(P, ssq_ap, b_ssq, rstd_ap, b_rstd, scale, K):
    P.op("act", lambda e: e.activation(out=rstd_ap, in_=ssq_ap, func=AF.Ln, scale=scale, bias=K.epst[:, 0:1]),
         reads=[b_ssq, K.b_eps], writes=[b_rstd])
    P.op("act", lambda e: e.activation(out=rstd_ap, in_=rstd_ap, func=AF.Exp, scale=-0.5),
         reads=[b_rstd], writes=[b_rstd])


def _phase1(K):
    nc, P = K.nc, K.P
    NSC, TSC = K.lim.get("p1_nsc", 8), 1024
    with _Arena(K) as A:
        xt = [A.sb(f"p1_xt{i}", [128, D], F32) for i in range(3)]
        b_xt = [Buf(f"p1_xt{i}") for i in range(3)]
        junk = A.sb("p1_junk", [128, D], BF16)
        b_junk = Buf("p1_junk")
        xb = [A.sb(f"p1_xb{i}", [128, 4, D], BF16) for i in range(2)]
        b_xb = [[Buf(f"p1_xb{i}_{j}") for j in range(4)] for i in range(2)]
        ssq = A.sb("p1_ssq", [128, 64], F32)
        rstd = A.sb("p1_rstd", [128, 64], F32)
        b_ssq = [Buf(f"p1_ssq{t}") for t in range(64)]
        b_rstd = [Buf(f"p1_rstd{t}") for t in range(64)]
        hT = [A.sb(f"p1_hT{i}", [128, DC, TSC], BF16) for i in range(2)]
        b_hT = [[Buf(f"p1_hT{i}_{h}") for h in range(2)] for i in range(2)]
        wsl = [A.sb(f"p1_w{i}", [128, DC, 512], BF16) for i in range(2)]
        b_w = [Buf(f"p1_w{i}") for i in range(2)]
        ropeC = [A.sb(f"p1_rc{i}", [128, TSC], F32) for i in range(2)]
        ropeS = [A.sb(f"p1_rs{i}", [128, TSC], F32) for i in range(2)]
        b_rope = [Buf(f"p1_rope{i}") for i in range(2)]
        rmat = A.sb("p1_rmat", [128, 128], BF16)
        b_rmat = Buf("p1_rmat")
        ob = [A.sb(f"p1_ob{i}", [128, TSC], BF16) for i in range(3)]
        b_ob = [Buf(f"p1_ob{i}") for i in range(3)]
        qb = [A.sb(f"p1_qb{i}", [128, 512], BF16) for i in range(3)]
        b_qb = [Buf(f"p1_qb{i}") for i in range(3)]
        t1 = [A.sb(f"p1_t1{i}", [128, 512], F32) for i in range(3)]
        t2 = [A.sb(f"p1_t2{i}", [128, 512], F32) for i in range(3)]
        b_t1 = [Buf(f"p1_t1{i}") for i in range(3)]
        b_t2 = [Buf(f"p1_t2{i}") for i in range(3)]
        psm = [A.ps(f"p1_psm{i}", [128, 512]) for i in range(3)]
        b_psm = [Buf(f"p1_psm{i}", excl=True) for i in range(3)]
        psr = [A.ps(f"p1_psr{i}", [128, 512]) for i in range(2)]
        b_psr = [Buf(f"p1_psr{i}", excl=True) for i in range(2)]
        pT = [A.ps(f"p1_pT{i}", [128, 8, 128], BF16) for i in range(2)]
        b_pT = [Buf(f"p1_pT{i}", excl=True) for i in range(2)]

        P.dma("pool", lambda e: e.dma_start(out=rmat[:, :], in_=K.c_rmat), K.b_in, b_rmat)

        cnt = {"xt": 0, "pT": 0, "psm": 0, "psr": 0, "ob": 0, "qb": 0, "ev": 0, "w": 0}

        def gen_hT(sc):
            slot = sc % 2
            for h in range(2):
                xs = (2 * sc + h) % 2
                for j in range(4):
                    tt = sc * 8 + h * 4 + j
                    xi = cnt["xt"] % 3
                    cnt["xt"] += 1
                    P.dma("sp", lambda e, xi=xi, tt=tt: e.dma_start(out=xt[xi][:, :], in_=K.x[tt * 128:(tt + 1) * 128, :]),
                          K.b_in, b_xt[xi])
                    P.op("act", lambda e, xi=xi, tt=tt: e.activation(out=junk[:, :], in_=xt[xi][:, :], func=AF.Square,
                                                                     accum_out=ssq[:, tt:tt + 1]),
                         reads=[b_xt[xi]], writes=[b_junk, b_ssq[tt]])
                    _rstd_chain(P, ssq[:, tt:tt + 1], b_ssq[tt], rstd[:, tt:tt + 1], b_rstd[tt], 1.0 / D, K)
                    P.op("dve", lambda e, xi=xi, tt=tt, xs=xs, j=j: e.tensor_scalar(
                        out=xb[xs][:, j, :], in0=xt[xi][:, :], scalar1=rstd[:, tt:tt + 1], scalar2=None, op0=ALU.mult),
                        reads=[b_xt[xi], b_rstd[tt]], writes=[b_xb[xs][j]])
                    yield
                for c in range(DC):
                    pi = cnt["pT"] % 2
                    cnt["pT"] += 1
                    for j in range(4):
                        P.op("pe", lambda e, pi=pi, j=j, c=c, xs=xs: e.transpose(
                            out=pT[pi][:, j, :], in_=xb[xs][:, j, c * 128:(c + 1) * 128], identity=K.ident[:, :]),
                            reads=[b_xb[xs][j], K.b_ident],
                            writes=[b_pT[pi]] if j == 0 else (), pwrites=() if j == 0 else [b_pT[pi]],
                            signal=True)
                    dst = hT[slot][:, c, h * 512:(h + 1) * 512]
                    src = pT[pi][:, 0:4, :]
                    if c % 2 == 0:
                        P.op("act", lambda e, dst=dst, src=src, c=c: e.activation(
                            out=dst.rearrange("p (j t) -> p j t", j=4), in_=src, func=AF.Copy, scale=K.g1t[:, c:c + 1]),
                            reads=[b_pT[pi], K.b_gains], writes=[b_hT[slot][h]] if c == 0 else (),
                            pwrites=() if c == 0 else [b_hT[slot][h]])
                    else:
                        P.op("dve", lambda e, dst=dst, src=src, c=c: e.tensor_scalar(
                            out=dst.rearrange("p (j t) -> p j t", j=4), in0=src, scalar1=K.g1t[:, c:c + 1], scalar2=None,
                            op0=ALU.mult),
                            reads=[b_pT[pi], K.b_gains], writes=(), pwrites=[b_hT[slot][h]])
                    yield

        def load_w(wg):
            wi = wg % 2
            P.dma("pool", lambda e, wi=wi, wg=wg: e.dma_start(
                out=wsl[wi][:, :, :], in_=K.w_in[:, (wg % 10) * 512:(wg % 10 + 1) * 512].rearrange("(c p) n -> p c n", p=128)),
                K.b_in, b_w[wi])

        def gen_proj(sc):
            pending = []
            slot = sc % 2
            rs = sc % 2
            P.dma("sp", lambda e: e.dma_start(out=ropeC[rs][:, :], in_=K.c_ropeC[:, sc * TSC:(sc + 1) * TSC]), K.b_in, b_rope[rs])
            P.dma("sp", lambda e: e.dma_start(out=ropeS[rs][:, :], in_=K.c_ropeS[:, sc * TSC:(sc + 1) * TSC]), K.b_in, b_rope[rs],
                  partial=True)
            for wgl in range(10):
                wg = sc * 10 + wgl
                wi = wg % 2
                if wg + 1 < NSC * 10:
                    load_w(wg + 1)
                for ft in range(4):
                    frow = (wgl * 4 + ft) * 128
                    is_rope = wgl < 6
                    oi = cnt["ob"] % 3
                    cnt["ob"] += 1
                    for h in range(2):
                        mi = cnt["psm"] % 3
                        cnt["psm"] += 1
                        terms = [(wsl[wi][:, c, ft * 128:(ft + 1) * 128], hT[slot][:, c, h * 512:(h + 1) * 512]) for c in range(DC)]
                        _mm_group(P, psm[mi][:, :], b_psm[mi], terms, reads=[b_w[wi], b_hT[slot][h]])
                        while pending:
                            pending.pop(0)()
                        odst = ob[oi][:, h * 512:(h + 1) * 512]
                        wkw = dict(writes=[b_ob[oi]]) if h == 0 else dict(pwrites=[b_ob[oi]])
                        if not is_rope:
                            cnt["ev"] += 1
                            if cnt["ev"] % 2 == 0:
                                P.op("act", lambda e, odst=odst, mi=mi: e.activation(out=odst, in_=psm[mi][:, :], func=AF.Copy),
                                     reads=[b_psm[mi]], **wkw)
                            else:
                                P.op("dve", lambda e, odst=odst, mi=mi: e.tensor_copy(out=odst, in_=psm[mi][:, :]),
                                     reads=[b_psm[mi]], **wkw)
                        else:
                            qi = cnt["qb"] % 3
                            cnt["qb"] += 1
                            ri = cnt["psr"] % 2
                            cnt["psr"] += 1
                            P.op("act", lambda e, qi=qi, mi=mi: e.activation(out=qb[qi][:, :], in_=psm[mi][:, :], func=AF.Copy),
                                 reads=[b_psm[mi]], writes=[b_qb[qi]])
                            P.op("dve", lambda e, qi=qi, mi=mi, h=h: e.tensor_tensor(
                                out=t1[qi][:, :], in0=psm[mi][:, :], in1=ropeC[rs][:, h * 512:(h + 1) * 512], op=ALU.mult),
                                reads=[b_psm[mi], b_rope[rs]], writes=[b_t1[qi]])

                            def tail(qi=qi, ri=ri, h=h, odst=odst, wkw=wkw):
                                P.op("pe", lambda e: e.matmul(psr[ri][:, :], lhsT=rmat[:, :], rhs=qb[qi][:, :], start=True, stop=True),
                                     reads=[b_qb[qi], b_rmat], writes=[b_psr[ri]])
                                P.op("dve", lambda e: e.tensor_tensor(
                                    out=t2[qi][:, :], in0=psr[ri][:, :], in1=ropeS[rs][:, h * 512:(h + 1) * 512], op=ALU.mult),
                                    reads=[b_psr[ri], b_rope[rs]], writes=[b_t2[qi]])
                                P.op("pool", lambda e: e.tensor_tensor(out=odst, in0=t1[qi][:, :], in1=t2[qi][:, :], op=ALU.add),
                                     reads=[b_t1[qi], b_t2[qi]], **wkw)
                            pending.append(tail)
                        yield
                    def store(oi=oi, frow=frow):
                        P.dma("sp", lambda e: e.dma_start(
                            out=K.qkvT[frow:frow + 128, sc * TSC:(sc + 1) * TSC], in_=ob[oi][:, :]),
                            b_ob[oi], K.b_qkvT, owner="src", partial=True)
                    pending.append(store)
            while pending:
                pending.pop(0)()

        load_w(0)
        for _ in gen_hT(0):
            pass
        for sc in range(NSC):
            _interleave(gen_proj(sc), gen_hT(sc + 1) if sc + 1 < NSC else None, 2)
    P.barrier()


def _phase_fnet(K):
    nc, P = K.nc, K.P
    NG = K.lim.get("pf_ng", 4)
    with _Arena(K) as A:
        CS = A.sb("pf_CS", [128, 2, 128], BF16)
        G = A.sb("pf_G", [128, 64, 2, 128], BF16)
        E = A.sb("pf_E", [128, 64], BF16)
        b_tab = Buf("pf_tab")
        P.dma("pool", lambda e: e.dma_start(out=CS[:, :, :], in_=K.c_dftCS), K.b_in, b_tab, partial=True)
        P.dma("pool", lambda e: e.dma_start(out=G[:, :, :, :], in_=K.c_dftG), K.b_in, b_tab, partial=True)
        P.dma("pool", lambda e: e.dma_start(out=E[:, :], in_=K.c_dftE), K.b_in, b_tab, partial=True)
        fT = [A.sb(f"pf_fT{i}", [128, S], BF16) for i in range(2)]
        b_fT = [Buf(f"pf_fT{i}") for i in range(2)]
        Zsb = [A.sb(f"pf_Z{i}", [128, 6, 128], BF16) for i in range(3)]
        b_Z = [Buf(f"pf_Z{i}") for i in range(3)]
        Bsb = [A.sb(f"pf_B{i}", [128, 2, 4, 128], BF16) for i in range(3)]
        b_B = [Buf(f"pf_B{i}") for i in range(3)]
        Bp = [A.sb(f"pf_Bp{i}", [128, 128, 128], BF16) for i in range(2)]
        b_Bp = [Buf(f"pf_Bp{i}") for i in range(2)]
        FTo = [A.sb(f"pf_FTo{i}", [128, S], BF16) for i in range(2)]
        b_FTo = [Buf(f"pf_FTo{i}") for i in range(2)]
        psZ = [A.ps(f"pf_psZ{i}", [128, 4, 128]) for i in range(2)]
        b_psZ = [Buf(f"pf_psZ{i}", excl=True) for i in range(2)]
        psB = [A.ps(f"pf_psB{i}", [128, 4, 128]) for i in range(2)]
        b_psB = [Buf(f"pf_psB{i}", excl=True) for i in range(2)]
        psC = [A.ps(f"pf_psC{i}", [128, 8, 64]) for i in range(2)]
        b_psC = [Buf(f"pf_psC{i}", excl=True) for i in range(2)]
        cnt = {"z": 0, "zs": 0, "b": 0, "bs": 0, "c": 0}

        def stage_a(g):
            fi = g % 2
            P.dma("sp", lambda e: e.dma_start(out=fT[fi][:, :], in_=K.qkvT[3 * AW + 128 * g:3 * AW + 128 * (g + 1), :]),
                  K.b_qkvT, b_fT[fi])
            for sp_ in range(32):
                zi = cnt["z"] % 2
                cnt["z"] += 1
                zs = cnt["zs"] % 3
                cnt["zs"] += 1
                for sl in range(2):
                    s2 = 2 * sp_ + sl
                    lhs = fT[fi][:, s2:S:64]
                    for ri in range(2):
                        first = (sl == 0 and ri == 0)
                        last = (sl == 1 and ri == 1)
                        P.op("pe", lambda e, zi=zi, ri=ri, sl=sl, lhs=lhs: e.matmul(psZ[zi][:, 2 * sl + ri, :], lhsT=lhs, rhs=CS[:, ri, :], start=True, stop=True),
                             reads=[b_fT[fi], b_tab] if (first or last) else (), writes=[b_psZ[zi]] if first else (),
                             pwrites=[b_psZ[zi]] if last else (), signal=(first or last))
                zv = Zsb[zs][:, :, :].rearrange("p (s k) c -> p s k c", s=2)
                pz = psZ[zi][:, :, :].rearrange("p (s r) c -> p s r c", s=2)
                P.op("act", lambda e, zv=zv, pz=pz: e.activation(out=zv[:, :, 0:2, :], in_=pz, func=AF.Copy),
                     reads=[b_psZ[zi]], writes=[b_Z[zs]])
                P.op("dve", lambda e, zv=zv, pz=pz: e.tensor_scalar(out=zv[:, :, 2, :], in0=pz[:, :, 0, :], scalar1=-1.0, scalar2=None, op0=ALU.mult),
                     reads=[b_psZ[zi]], pwrites=[b_Z[zs]])
                bi = cnt["b"] % 2
                cnt["b"] += 1
                mm = []
                for sl in range(2):
                    s2 = 2 * sp_ + sl
                    mm.append((psB[bi][:, 2 * sl + 0, :], G[:, s2, 0, :], zv[:, sl, 0, :], True, False))
                    mm.append((psB[bi][:, 2 * sl + 0, :], G[:, s2, 1, :], zv[:, sl, 1, :], False, True))
                    mm.append((psB[bi][:, 2 * sl + 1, :], G[:, s2, 0, :], zv[:, sl, 1, :], True, False))
                    mm.append((psB[bi][:, 2 * sl + 1, :], G[:, s2, 1, :], zv[:, sl, 2, :], False, True))
                for i, (o, l, r, st, sp2) in enumerate(mm):
                    first, last = (i == 0), (i == len(mm) - 1)
                    P.op("pe", lambda e, o=o, l=l, r=r, st=st, sp2=sp2: e.matmul(o, lhsT=l, rhs=r, start=st, stop=sp2),
                         reads=[b_Z[zs], b_tab] if (first or last) else (), writes=[b_psB[bi]] if first else (),
                         pwrites=[b_psB[bi]] if last else (), signal=(first or last))
                if sp_ % 2 == 0:
                    cnt["bs"] += 1
                bs = cnt["bs"] % 3
                half = sp_ % 2
                wkw = dict(writes=[b_B[bs]]) if half == 0 else dict(pwrites=[b_B[bs]])
                dstB = Bsb[bs][:, :, 2 * half:2 * half + 2, :]
                srcB = psB[bi][:, :, :].rearrange("p (s r) c -> p r s c", s=2)
                if sp_ % 2 == 0:
                    P.op("act", lambda e, dstB=dstB, srcB=srcB: e.activation(out=dstB, in_=srcB, func=AF.Copy), reads=[b_psB[bi]], **wkw)
                else:
                    P.op("dve", lambda e, dstB=dstB, srcB=srcB: e.tensor_copy(out=dstB, in_=srcB), reads=[b_psB[bi]], **wkw)
                if half == 1:
                    s0 = 2 * sp_ - 2
                    for ri in range(2):
                        P.dma("sp", lambda e, bs=bs, s0=s0, g=g, ri=ri: e.dma_start(
                            out=K.B_s[g, ri, s0:s0 + 4].rearrange("s k c -> k s c"), in_=Bsb[bs][:, ri, :, :]),
                            b_B[bs], K.b_Bs[g], owner="src", partial=True)

        def stage_c(g):
            pi = g % 2
            P.dma("sp", lambda e: e.dma_start(out=Bp[pi][:, :, :], in_=K.B_s[g].rearrange("r s k c -> (r s) k c")),
                  K.b_Bs[g], b_Bp[pi])
            fo = FTo[pi][:, :].rearrange("p (k2 k1) -> p k1 k2", k1=128)
            for kb in range(16):
                ci = cnt["c"] % 2
                cnt["c"] += 1
                for kl in range(8):
                    k1 = kb * 8 + kl
                    P.op("pe", lambda e, ci=ci, kl=kl, k1=k1: e.matmul(psC[ci][:, kl, :], lhsT=Bp[pi][:, k1, :], rhs=E[:, :], start=True, stop=True),
                         reads=[b_Bp[pi], b_tab], writes=[b_psC[ci]] if kl == 0 else (), pwrites=() if kl == 0 else [b_psC[ci]],
                         signal=(kl == 0 or kl == 7))
                wkw = dict(writes=[b_FTo[pi]]) if kb == 0 else dict(pwrites=[b_FTo[pi]])
                if kb % 2 == 0:
                    P.op("act", lambda e, ci=ci, kb=kb: e.activation(out=fo[:, kb * 8:(kb + 1) * 8, :], in_=psC[ci][:, :, :], func=AF.Copy),
                         reads=[b_psC[ci]], **wkw)
                else:
                    P.op("dve", lambda e, ci=ci, kb=kb: e.tensor_copy(out=fo[:, kb * 8:(kb + 1) * 8, :], in_=psC[ci][:, :, :]),
                         reads=[b_psC[ci]], **wkw)
            P.dma("sp", lambda e: e.dma_start(out=K.FT_s[128 * g:128 * (g + 1), :], in_=FTo[pi][:, :]),
                  b_FTo[pi], K.b_FT, owner="src", partial=True)

        order = []
        for g in range(NG):
            order.append(("a", g))
            if g >= 1:
                order.append(("c", g - 1))
        order.append(("c", NG - 1))
        for kind, g in order:
            (stage_a if kind == "a" else stage_c)(g)
    P.barrier()


_D1_LO = [0, 0, 8, 16, 24]
_D1_W = [8, 16, 16, 16, 8]
_D1_OFF = [0, 32, 96, 160, 224]


def _precast_blocks():
    blocks = []
    for c in range(DC):
        blocks.append(("out", c, 0, D))
    for c in range(DC):
        for q in range(4):
            blocks.append(("up", c, q * 2816, 2816))
    for c in range(NFF):
        blocks.append(("down", c, 0, D))
    return blocks


def _phase_attn(K):
    nc, P = K.nc, K.P
    NHP = K.lim.get("p2_nhp", 12)
    NQT = K.lim.get("p2_nq", 64)
    W = S + 2 * PAD
    with _Arena(K) as A:
        qT = [A.sb(f"p2_qT{i}", [128, S], BF16) for i in range(2)]
        b_qT = [Buf(f"p2_qT{i}") for i in range(2)]
        kT = [A.sb(f"p2_kT{i}", [128, W], BF16) for i in range(2)]
        b_kT = [Buf(f"p2_kT{i}") for i in range(2)]
        vT = A.sb("p2_vT", [128, W], BF16)
        b_vT = Buf("p2_vT")
        V1 = A.sb("p2_V1", [128, 65, 2, 65], BF16)
        V4 = A.sb("p2_V4", [128, 68, 2, 65], BF16)
        V16 = A.sb("p2_V16", [128, 80, 2, 65], BF16)
        b_V = Buf("p2_V")
        mA = A.sb("p2_mA", [128, 4, 512], BF16)
        mB = A.sb("p2_mB", [128, 4, 256], BF16)
        b_mask = Buf("p2_mask")
        PT = [A.sb(f"p2_PT{i}", [128, 768], BF16) for i in range(2)]
        b_PT = [Buf(f"p2_PT{i}") for i in range(2)]
        PTm = [A.sb(f"p2_PTm{i}", [128, 512], BF16) for i in range(4)]
        Pd1 = [A.sb(f"p2_Pd1{i}", [128, 5, 128], BF16) for i in range(4)]
        b_PTm = [Buf(f"p2_PTm{i}") for i in range(4)]
        rec = [A.sb(f"p2_rec{i}", [128, 2], F32) for i in range(2)]
        b_rec = [Buf(f"p2_rec{i}") for i in range(2)]
        apair = [A.sb(f"p2_ap{i}", [128, 2, 64], BF16) for i in range(2)]
        b_apair = [Buf(f"p2_ap{i}") for i in range(2)]
        aT = A.sb("p2_aT", [128, S], BF16)
        b_aT = Buf("p2_aT")
        stg = [A.sb(f"p2_stg{i}", [128, 2816], BF16) for i in range(3)]
        b_stg = [Buf(f"p2_stg{i}") for i in range(3)]
        Sps = [A.ps(f"p2_S{i}", [128, 1024]) for i in range(2)]
        b_S = [Buf(f"p2_S{i}", excl=True) for i in range(2)]
        Ops = [A.ps(f"p2_O{i}", [128, 512]) for i in range(2)]
        b_O = [Buf(f"p2_O{i}", excl=True) for i in range(2)]
        Tps = [A.ps(f"p2_T{i}", [128, 8, 128], BF16) for i in range(2)]
        b_T = [Buf(f"p2_T{i}", excl=True) for i in range(2)]

        P.dma("pool", lambda e: e.dma_start(out=mA[:, :, :], in_=K.c_maskA), K.b_in, b_mask, partial=True)
        P.dma("pool", lambda e: e.dma_start(out=mB[:, :, :], in_=K.c_maskB), K.b_in, b_mask, partial=True)
        for i in range(2):
            P.op("pool", lambda e, i=i: e.memset(kT[i][:, 0:PAD], 0.0), writes=[b_kT[i]])
            P.op("pool", lambda e, i=i: e.memset(kT[i][:, PAD + S:W], 0.0), pwrites=[b_kT[i]])
        P.op("pool", lambda e: e.memset(vT[:, 0:PAD], 0.0), writes=[b_vT])
        P.op("pool", lambda e: e.memset(vT[:, PAD + S:W], 0.0), pwrites=[b_vT])
        for i in range(4):
            P.op("pool", lambda e, i=i: e.memset(Pd1[i][:, :, :], 0.0), writes=[b_PTm[i]])
        if NQT < 64:
            P.op("pool", lambda e: e.memset(aT[:, :], 0.0), writes=[b_aT])
        first = True
        for (Vt, ncls, nt) in ((V1, 1, 65), (V4, 4, 17), (V16, 16, 5)):
            P.op("dve", lambda e, Vt=Vt: e.memset(Vt[:, :, :, :], 0.0), writes=[b_V] if first else (), pwrites=() if first else [b_V])
            first = False
        for (Vt, ncls, nt) in ((V1, 1, 65), (V4, 4, 17), (V16, 16, 5)):
            Vv = Vt[:, :, :, :].rearrange("p (c n) h d -> p c n h d", n=nt)
            P.op("dve", lambda e, Vv=Vv, nt=nt: e.memset(Vv[:, :, 1:nt - 1, :, 64:65], 1.0), pwrites=[b_V], reads=[b_V])
            P.op("dve", lambda e, Vv=Vv: e.memset(Vv[64:128, :, 0:1, :, 64:65], 1.0), pwrites=[b_V], reads=[b_V])
            P.op("dve", lambda e, Vv=Vv, nt=nt: e.memset(Vv[0:64, :, nt - 1:nt, :, 64:65], 1.0), pwrites=[b_V], reads=[b_V])

        blocks = _precast_blocks() if K.lim.get("p2_precast", True) else []
        per_hp = (len(blocks) + NHP - 1) // NHP
        cnt = {"stg": 0, "T": 0, "ev": 0, "blk": 0}

        def precast_one():
            if cnt["blk"] >= len(blocks):
                return
            kind, c, c0, ncol = blocks[cnt["blk"]]
            cnt["blk"] += 1
            si = cnt["stg"] % 3
            cnt["stg"] += 1
            src = {"out": K.w_out, "up": K.w_up, "down": K.w_down}[kind]
            dst = {"out": K.wout_s, "up": K.wup_s, "down": K.wdown_s}[kind]
            bdst = {"out": K.b_wout, "up": K.b_wup, "down": K.b_wdown}[kind]
            P.dma("pool", lambda e: e.dma_start(out=stg[si][:, 0:ncol], in_=src[c * 128:(c + 1) * 128, c0:c0 + ncol]), K.b_in, b_stg[si])
            if kind == "up":
                t0 = c0 // 128
                P.dma("sp", lambda e: e.dma_start(out=K.wup_s[t0:t0 + 22, :, c, :].rearrange("t p n -> p t n"),
                                                  in_=stg[si][:, 0:ncol].rearrange("p (t n) -> p t n", n=128)),
                      b_stg[si], bdst, owner="src", partial=True)
            else:
                P.dma("sp", lambda e: e.dma_start(out=dst[c * 128:(c + 1) * 128, c0:c0 + ncol], in_=stg[si][:, 0:ncol]), b_stg[si], bdst, owner="src", partial=True)

        def load_qk(hp):
            i = hp % 2
            P.dma("sp", lambda e: e.dma_start(out=qT[i][:, :], in_=K.qkvT[128 * hp:128 * (hp + 1), :]), K.b_qkvT, b_qT[i])
            P.dma("sp", lambda e: e.dma_start(out=kT[i][:, PAD:PAD + S], in_=K.qkvT[AW + 128 * hp:AW + 128 * (hp + 1), :]), K.b_qkvT, b_kT[i])

        def load_v(hp):
            P.dma("sp", lambda e: e.dma_start(out=vT[:, PAD:PAD + S], in_=K.qkvT[2 * AW + 128 * hp:2 * AW + 128 * (hp + 1), :]), K.b_qkvT, b_vT)

        def kcols(pat, cls, n):
            if pat == 1:
                return PAD - 64 + 128 * n, 1
            if pat == 4:
                return PAD - 256 + 512 * n + cls, 4
            return PAD - 1024 + 2048 * n + cls, 16

        def v_arrange():
            jobs = []
            for n in range(65):
                jobs.append((V1, n, kcols(1, 0, n)))
            for r in range(4):
                for n in range(17):
                    jobs.append((V4, r * 17 + n, kcols(4, r, n)))
            for c in range(16):
                for n in range(5):
                    jobs.append((V16, c * 5 + n, kcols(16, c, n)))
            i = 0
            while i < len(jobs):
                Vt, t0, _ = jobs[i]
                grp = [jobs[i]]
                while len(grp) < 8 and i + len(grp) < len(jobs) and jobs[i + len(grp)][0] is Vt:
                    grp.append(jobs[i + len(grp)])
                ti = cnt["T"] % 2
                cnt["T"] += 1
                for gi, (_, tidx, (c0, st)) in enumerate(grp):
                    src = vT[:, c0:c0 + 127 * st + 1:st]
                    P.op("pe", lambda e, ti=ti, gi=gi, src=src: e.transpose(out=Tps[ti][:, gi, :], in_=src, identity=K.ident[:, :]),
                         reads=[b_vT, K.b_ident], writes=[b_T[ti]] if gi == 0 else (), pwrites=() if gi == 0 else [b_T[ti]],
                         signal=(gi == 0 or gi == len(grp) - 1))
                ng = len(grp)
                dst = Vt[:, t0:t0 + ng, :, 0:64]
                srcp = Tps[ti][:, 0:ng, :].rearrange("p g (h d) -> p g h d", h=2)
                cnt["ev"] += 1
                if cnt["ev"] % 2 == 0:
                    P.op("act", lambda e, dst=dst, srcp=srcp: e.activation(out=dst, in_=srcp, func=AF.Copy), reads=[b_T[ti]], pwrites=[b_V])
                else:
                    P.op("dve", lambda e, dst=dst, srcp=srcp: e.tensor_copy(out=dst, in_=srcp), reads=[b_T[ti]], pwrites=[b_V])
                i += ng

        def qcols(qs, a, r, hh, lo=0, hi=32):
            v = qT[qs][64 * hh:64 * hh + 64, 512 * a:512 * a + 512].rearrange("p (i u r) -> p r u i", u=4, r=4)
            return v[:, r, :, lo:hi]

        def score_mms(hp, qi, hh):
            a, r = qi // 4, qi % 4
            w = a % 4
            ks = hp % 2
            Sb = Sps[hh]
            kk = kT[ks]
            rows = slice(64 * hh, 64 * hh + 64)
            mms = []
            for t in range(2):
                c0, st = kcols(4, r, a + t)
                mms.append((Sb[:, t * 128:(t + 1) * 128], kk[rows, c0:c0 + 127 * st + 1:st], qcols(ks, a, r, hh)))
            n16 = a // 4
            for u in range(4):
                c16 = 4 * u + r
                qv = qT[ks][rows, 512 * a + c16:512 * a + c16 + 31 * 16 + 1:16]
                for k in range(2):
                    c0, st = kcols(16, c16, n16 + k)
                    o0 = 256 + (2 * u + k) * 32
                    mms.append((Sb[:, o0:o0 + 32], kk[rows, c0:c0 + 127 * st + 1:st], qv))
            for m in range(5):
                c0, st = kcols(1, 0, 4 * a + m)
                o0 = 512 + _D1_OFF[m]
                wd = _D1_W[m]
                mms.append((Sb[:, o0:o0 + 4 * wd].rearrange("p (u i) -> p u i", u=4), kk[rows, c0:c0 + 128],
                            qcols(ks, a, r, hh, _D1_LO[m], _D1_LO[m] + wd)))
            return mms

        def score_post(hp, qi, hh):
            a, r = qi // 4, qi % 4
            w = a % 4
            Sb = Sps[hh]
            P.op("act", lambda e: e.activation(out=PT[hh][:, :], in_=Sb[:, 0:768], func=AF.Exp, scale=0.125),
                 reads=[b_S[hh]], writes=[b_PT[hh]])
            pm = 2 * (qi % 2) + hh
            eng = "dve" if hh == 0 else "pool"
            P.op(eng, lambda e: e.tensor_tensor(out=PTm[pm][:, :], in0=PT[hh][:, 0:512], in1=mA[:, w, :], op=ALU.mult),
                 reads=[b_PT[hh], b_mask], writes=[b_PTm[pm]])
            src = PT[hh][:, 512 + 32:512 + 224].rearrange("p (m u i) -> p m u i", m=3, u=4)
            msk = mB[:, r, 32:224].rearrange("p (m u i) -> p m u i", m=3, u=4)
            P.op(eng, lambda e: e.tensor_tensor(out=_pd1_mid(Pd1[pm]), in0=src, in1=msk, op=ALU.mult),
                 reads=[b_PT[hh], b_mask], pwrites=[b_PTm[pm]])
            P.op(eng, lambda e: e.tensor_tensor(out=_pd1_edge(Pd1[pm]), in0=_pt_edge(PT[hh]), in1=_mb_edge(mB, r), op=ALU.mult),
                 reads=[b_PT[hh], b_mask], pwrites=[b_PTm[pm]])

        def score(hp, qi):
            ks = hp % 2
            m0 = score_mms(hp, qi, 0)
            m1 = score_mms(hp, qi, 1)
            n = len(m0)
            for i in range(n):
                for hh, mm in ((0, m0), (1, m1)):
                    o, l, rr = mm[i]
                    P.op("pe", lambda e, o=o, l=l, rr=rr: e.matmul(o, lhsT=l, rhs=rr, start=True, stop=True),
                         reads=[b_kT[ks], b_qT[ks]] if (i == 0 or i == n - 1) else (),
                         writes=[b_S[hh]] if i == 0 else (), pwrites=[b_S[hh]] if i == n - 1 else (),
                         signal=(i == 0 or i == n - 1))
            for hh in range(2):
                score_post(hp, qi, hh)

        def pv(hp, qi):
            a, r = qi // 4, qi % 4
            oi = qi % 2
            for hh in range(2):
                pm = 2 * (qi % 2) + hh
                o = Ops[oi][:, 65 * hh:65 * hh + 65]
                mms = []
                for t in range(2):
                    mms.append((o, PTm[pm][:, t * 128:(t + 1) * 128], V4[:, r * 17 + a + t, hh, :]))
                n16 = a // 4
                for k in range(2):
                    for u in range(4):
                        c16 = 4 * u + r
                        o0 = 256 + (2 * u + k) * 32
                        mms.append((Ops[oi][32 * u:32 * u + 32, 65 * hh:65 * hh + 65], PTm[pm][:, o0:o0 + 32], V16[:, c16 * 5 + n16 + k, hh, :]))
                for m in range(5):
                    mms.append((o, Pd1[pm][:, m, :], V1[:, 4 * a + m, hh, :]))
                n = len(mms)
                for i, (oo, l, rr) in enumerate(mms):
                    tp = (0, 32 * ((i - 2) % 4)) if 2 <= i < 10 else None
                    P.op("pe", lambda e, oo=oo, l=l, rr=rr, i=i, n=n, tp=tp: e.matmul(oo, lhsT=l, rhs=rr, start=(i == 0), stop=(i == n - 1), tile_position=tp),
                         reads=[b_PTm[pm], b_V] if (i == 0 or i == n - 1) else (),
                         writes=[b_O[oi]] if (i == 0 and hh == 0) else (),
                         pwrites=[b_O[oi]] if (i == n - 1 or (i == 0 and hh == 1)) else (),
                         signal=(i == 0 or i == n - 1))
            ov = Ops[oi][:, 0:130].rearrange("p (h d) -> p h d", h=2)
            P.op("dve", lambda e: e.reciprocal(out=rec[oi][:, :], in_=ov[:, :, 64]), reads=[b_O[oi]], writes=[b_rec[oi]])
            P.op("dve", lambda e: e.tensor_scalar(out=apair[oi][:, 0, :], in0=ov[:, 0, 0:64], scalar1=rec[oi][:, 0:1], scalar2=None, op0=ALU.mult),
                 reads=[b_O[oi], b_rec[oi]], writes=[b_apair[oi]])
            P.op("act", lambda e: e.activation(out=apair[oi][:, 1, :], in_=ov[:, 1, 0:64], func=AF.Copy, scale=rec[oi][:, 1:2]),
                 reads=[b_O[oi], b_rec[oi]], pwrites=[b_apair[oi]])

        def tr(hp, qi):
            a, r = qi // 4, qi % 4
            oi = qi % 2
            ti = cnt["T"] % 2
            cnt["T"] += 1
            P.op("pe", lambda e: e.transpose(out=Tps[ti][:, 0, :], in_=apair[oi][:, :, :].rearrange("p h d -> p (h d)"), identity=K.ident[:, :]),
                 reads=[b_apair[oi], K.b_ident], writes=[b_T[ti]])
            dst = aT[:, 512 * a:512 * a + 512].rearrange("p (i u r) -> p r u i", u=4, r=4)[:, r, :, :]
            srcp = Tps[ti][:, 0, :].rearrange("p (u i) -> p u i", u=4)
            wkw = dict(writes=[b_aT]) if (qi == 0 and NQT == 64) else dict(pwrites=[b_aT])
            cnt["ev"] += 1
            if cnt["ev"] % 2 == 0:
                P.op("act", lambda e: e.activation(out=dst, in_=srcp, func=AF.Copy), reads=[b_T[ti]], **wkw)
            else:
                P.op("dve", lambda e: e.tensor_copy(out=dst, in_=srcp), reads=[b_T[ti]], **wkw)

        load_qk(0)
        load_v(0)
        for hp in range(NHP):
            v_arrange()
            if hp + 1 < NHP:
                load_qk(hp + 1)
                load_v(hp + 1)
            score(hp, 0)
            nblk = 0
            for qi in range(NQT):
                if qi + 1 < NQT:
                    score(hp, qi + 1)
                pv(hp, qi)
                if qi >= 1:
                    tr(hp, qi - 1)
                if (qi * per_hp) // NQT >= nblk and nblk < per_hp:
                    precast_one()
                    nblk += 1
            tr(hp, NQT - 1)
            P.dma("sp", lambda e, hp=hp: e.dma_start(out=K.attnT_s[128 * hp:128 * (hp + 1), :], in_=aT[:, :]), b_aT, K.b_attnT,
                  owner="src", partial=True)
        while cnt["blk"] < len(blocks):
            precast_one()
    P.barrier()


def _cap(base_ap, dims):
    return bass.AP(base_ap.tensor, base_ap.offset, [list(base_ap.ap[0])] + [list(d) for d in dims])


def _pd1_mid(t):
    return _cap(t[:, 1, 0:1], [(136, 3), (32, 4), (1, 16)])


def _pd1_edge(t):
    return _cap(t[:, 0, 0:1], [(4 * 128 + 24, 2), (32, 4), (1, 8)])


def _pt_edge(pt):
    return _cap(pt[:, 512:513], [(224, 2), (8, 4), (1, 8)])


def _mb_edge(mB, r):
    return _cap(mB[:, r, 0:1], [(224, 2), (8, 4), (1, 8)])


def _phase3a(K):
    nc, P = K.nc, K.P
    NG = K.lim.get("p3a_ng", 16)
    with _Arena(K) as A:
        wout = A.sb("p3a_wout", [128, DC, D], BF16)
        b_wo = Buf("p3a_wout")
        for q in range(4):
            P.dma("sp", lambda e, q=q: e.dma_start(out=wout[:, 4 * q:4 * q + 4, :],
                                                   in_=K.wout_s[512 * q:512 * (q + 1), :].rearrange("(c p) n -> p c n", p=128)),
                  K.b_wout, b_wo, partial=True)
        for c in range(DC):
            eng = "dve" if c % 2 == 0 else "act"
            if eng == "dve":
                P.op("dve", lambda e, c=c: e.tensor_scalar(out=wout[:, c, :], in0=wout[:, c, :], scalar1=K.gmt[:, c:c + 1], scalar2=None, op0=ALU.mult),
                     reads=[b_wo, K.b_gains], pwrites=[b_wo])
            else:
                P.op("act", lambda e, c=c: e.activation(out=wout[:, c, :], in_=wout[:, c, :], func=AF.Copy, scale=K.gmt[:, c:c + 1]),
                     reads=[b_wo, K.b_gains], pwrites=[b_wo])
        aF = [A.sb(f"p3a_aF{i}", [128, DC, 512], BF16) for i in range(2)]
        b_aF = [Buf(f"p3a_aF{i}") for i in range(2)]
        sq = A.sb("p3a_sq", [128, DC, 512], BF16)
        b_sq = Buf("p3a_sq")
        ones = A.sb("p3a_ones", [128, 2], BF16)
        b_ones = Buf("p3a_ones")
        P.op("dve", lambda e: e.memset(ones[:, 0:1], 1.0 / AW), writes=[b_ones])
        P.op("dve", lambda e: e.memset(ones[:, 1:2], 1.0 / FW), pwrites=[b_ones])
        xt = [A.sb(f"p3a_xt{i}", [128, D], F32) for i in range(2)]
        b_xt = [Buf(f"p3a_xt{i}") for i in range(2)]
        x1t = [A.sb(f"p3a_x1t{i}", [128, D], F32) for i in range(2)]
        b_x1t = [Buf(f"p3a_x1t{i}") for i in range(2)]
        junk = A.sb("p3a_junk", [128, D], BF16)
        b_junk = Buf("p3a_junk")
        xb2 = A.sb("p3a_xb2", [128, 4, D], BF16)
        b_xb2 = [Buf(f"p3a_xb2_{j}") for j in range(4)]
        h2st = [A.sb(f"p3a_h2st{i}", [128, DC, 512], BF16) for i in range(2)]
        b_h2st = [Buf(f"p3a_h2st{i}") for i in range(2)]
        rs_af = A.sb("p3a_rsaf", [128, 4, 2], F32)
        b_rsaf = Buf("p3a_rsaf")
        ssq2 = A.sb("p3a_ssq2", [128, 4], F32)
        rs2 = A.sb("p3a_rs2", [128, 4], F32)
        b_ssq2 = [Buf(f"p3a_ssq2_{j}") for j in range(4)]
        b_rs2 = [Buf(f"p3a_rs2_{j}") for j in range(4)]
        zt = A.sb("p3a_zt", [128, DC, 1], BF16)
        b_zt = Buf("p3a_zt")
        Aps = [A.ps(f"p3a_A{i}", [128, 512]) for i in range(2)]
        b_A = [Buf(f"p3a_A{i}", excl=True) for i in range(2)]
        Fps = [A.ps(f"p3a_F{i}", [128, 512]) for i in range(2)]
        b_F = [Buf(f"p3a_F{i}", excl=True) for i in range(2)]
        Tps = [A.ps(f"p3a_T{i}", [128, 8, 128], BF16) for i in range(2)]
        b_T = [Buf(f"p3a_T{i}", excl=True) for i in range(2)]
        ssp = A.ps("p3a_ssp", [128, 4, 2])
        b_ssp = Buf("p3a_ssp", excl=True)
        cnt = {"x": 0, "af": 0, "T": 0, "ev": 0}

        P.op("dve", lambda e: e.memset(zt[:, :, :], 0.0), writes=[b_zt])
        h2v = K.h2T_s.rearrange("(c p) t -> p c t", p=128)
        P.dma("sp", lambda e: e.dma_start(out=h2v[:, :, 0:1], in_=zt[:, :, :], allow_slow_non_contiguous=True), b_zt, K.b_h2T, owner="src", partial=True)
        P.dma("sp", lambda e: e.dma_start(out=h2v[:, :, S + 1:S + 2], in_=zt[:, :, :], allow_slow_non_contiguous=True), b_zt, K.b_h2T, owner="src", partial=True)

        def load_aF(gi):
            si = gi % 2
            P.dma("sp", lambda e: e.dma_start(out=aF[si][:, 0:12, :],
                                              in_=K.attnT_s[:, 512 * gi:512 * (gi + 1)].rearrange("(c p) t -> p c t", p=128)),
                  K.b_attnT, b_aF[si])
            P.dma("sp", lambda e: e.dma_start(out=aF[si][:, 12:16, :],
                                              in_=K.FT_s[:, 512 * gi:512 * (gi + 1)].rearrange("(c p) t -> p c t", p=128)),
                  K.b_FT, b_aF[si], partial=True)

        rs_af2 = [rs_af, A.sb("p3a_rsaf1", [128, 4, 2], F32)]
        b_rsaf2 = [b_rsaf, Buf("p3a_rsaf1")]

        def stats(gi):
            si = gi % 2
            P.op("act", lambda e: e.activation(out=sq[:, :, :], in_=aF[si][:, :, :], func=AF.Square),
                 reads=[b_aF[si]], writes=[b_sq])
            for j in range(4):
                for br, (c0, c1) in enumerate(((0, 12), (12, 16))):
                    for c in range(c0, c1):
                        first, last = (c == c0), (c == c1 - 1)
                        P.op("pe", lambda e, j=j, c=c, br=br, first=first, last=last: e.matmul(
                            ssp[:, j, br:br + 1], lhsT=sq[:, c, 128 * j:128 * (j + 1)], rhs=ones[:, br:br + 1], start=first, stop=last),
                            reads=[b_sq, b_ones] if (first or last) else (),
                            writes=[b_ssp] if (first and j == 0 and br == 0) else (),
                            pwrites=[b_ssp] if (last or (first and not (j == 0 and br == 0))) else (),
                            signal=(first or last))
            _rstd_chain(P, ssp[:, :, :], b_ssp, rs_af2[si][:, :, :], b_rsaf2[si], 1.0, K)

        def load_x(tt):
            xi = tt % 2
            P.dma("sp", lambda e: e.dma_start(out=xt[xi][:, :], in_=K.x[128 * tt:128 * (tt + 1), :]), K.b_in, b_xt[xi])

        load_aF(0)
        stats(0)
        load_x(0)
        for gi in range(NG):
            si = gi % 2
            if gi + 1 < NG:
                load_aF(gi + 1)
            for j in range(4):
                tt = 4 * gi + j
                xi = tt % 2
                if tt + 1 < 4 * NG:
                    load_x(tt + 1)
                for cg in range(4):
                    ai = cnt["af"] % 2
                    cnt["af"] += 1
                    _mm_group(P, Aps[ai][:, :], b_A[ai],
                              [(aF[si][:, c, 128 * j:128 * (j + 1)], wout[:, c, 512 * cg:512 * (cg + 1)]) for c in range(12)],
                              reads=[b_aF[si], b_wo])
                    _mm_group(P, Fps[ai][:, :], b_F[ai],
                              [(aF[si][:, c, 128 * j:128 * (j + 1)], wout[:, c, 512 * cg:512 * (cg + 1)]) for c in range(12, 16)],
                              reads=[b_aF[si], b_wo])
                    if j == 0 and cg == 1 and gi + 1 < NG:
                        stats(gi + 1)
                    dst = x1t[xi][:, 512 * cg:512 * (cg + 1)]
                    P.op("dve", lambda e, ai=ai, xi=xi, j=j, cg=cg, dst=dst, si=si: e.scalar_tensor_tensor(
                        out=dst, in0=Aps[ai][:, :], scalar=rs_af2[si][:, j, 0:1], in1=xt[xi][:, 512 * cg:512 * (cg + 1)], op0=ALU.mult, op1=ALU.add),
                        reads=[b_A[ai], b_rsaf2[si], b_xt[xi]], **(dict(writes=[b_x1t[xi]]) if cg == 0 else dict(pwrites=[b_x1t[xi]])))
                    P.op("dve", lambda e, ai=ai, j=j, dst=dst, si=si: e.scalar_tensor_tensor(
                        out=dst, in0=Fps[ai][:, :], scalar=rs_af2[si][:, j, 1:2], in1=dst, op0=ALU.mult, op1=ALU.add),
                        reads=[b_F[ai], b_rsaf2[si], b_x1t[xi]], pwrites=[b_x1t[xi]])
                P.op("act", lambda e, xi=xi, j=j: e.activation(out=junk[:, :], in_=x1t[xi][:, :], func=AF.Square, accum_out=ssq2[:, j:j + 1]),
                     reads=[b_x1t[xi]], writes=[b_junk, b_ssq2[j]])
                _rstd_chain(P, ssq2[:, j:j + 1], b_ssq2[j], rs2[:, j:j + 1], b_rs2[j], 1.0 / D, K)
                P.op("dve", lambda e, xi=xi, j=j: e.tensor_scalar(out=xb2[:, j, :], in0=x1t[xi][:, :], scalar1=rs2[:, j:j + 1], scalar2=None, op0=ALU.mult),
                     reads=[b_x1t[xi], b_rs2[j]], writes=[b_xb2[j]])
                P.dma("sp", lambda e, xi=xi, tt=tt: e.dma_start(out=K.x1_s[128 * tt:128 * (tt + 1), :], in_=x1t[xi][:, :]),
                      b_x1t[xi], K.b_x1, owner="src", partial=True)
            hi = gi % 2
            for c in range(DC):
                ti = cnt["T"] % 2
                cnt["T"] += 1
                for j in range(4):
                    P.op("pe", lambda e, ti=ti, j=j, c=c: e.transpose(out=Tps[ti][:, j, :], in_=xb2[:, j, 128 * c:128 * (c + 1)], identity=K.ident[:, :]),
                         reads=[b_xb2[j], K.b_ident], writes=[b_T[ti]] if j == 0 else (), pwrites=() if j == 0 else [b_T[ti]])
                dst = h2st[hi][:, c, :].rearrange("p (j t) -> p j t", j=4)
                wkw = dict(writes=[b_h2st[hi]]) if c == 0 else dict(pwrites=[b_h2st[hi]])
                if c % 2 == 0:
                    P.op("act", lambda e, ti=ti, c=c, dst=dst: e.activation(out=dst, in_=Tps[ti][:, 0:4, :], func=AF.Copy, scale=K.g2t[:, c:c + 1]),
                         reads=[b_T[ti], K.b_gains], **wkw)
                else:
                    P.op("dve", lambda e, ti=ti, c=c, dst=dst: e.tensor_scalar(out=dst, in0=Tps[ti][:, 0:4, :], scalar1=K.g2t[:, c:c + 1], scalar2=None, op0=ALU.mult),
                         reads=[b_T[ti], K.b_gains], **wkw)
            P.dma("sp", lambda e, hi=hi, gi=gi: e.dma_start(out=h2v[:, :, 1 + 512 * gi:1 + 512 * (gi + 1)], in_=h2st[hi][:, :, :]),
                  b_h2st[hi], K.b_h2T, owner="src", partial=True)
    P.barrier()


def _phase3b(K):
    nc, P = K.nc, K.P
    CH = 510
    nchunks_all = (S + CH - 1) // CH
    NCH = K.lim.get("p3b_nch", nchunks_all)
    with _Arena(K) as A:
        fgt = A.sb("p3b_fg", [128, D], F32)
        cwt = A.sb("p3b_cw", [128, 2 * NFF, 3], F32)
        cbt = A.sb("p3b_cb", [128, 2 * NFF], F32)
        b_c = Buf("p3b_consts")
        P.dma("sp", lambda e: e.dma_start(out=fgt[:, :], in_=K.fg), K.b_in, b_c, partial=True)
        P.dma("sp", lambda e: e.dma_start(out=cwt[:, :, :], in_=K.cw), K.b_in, b_c, partial=True)
        P.dma("sp", lambda e: e.dma_start(out=cbt[:, :], in_=K.cb), K.b_in, b_c, partial=True)
        h2c = [A.sb(f"p3b_h2c{i}", [128, DC, 512], BF16) for i in range(2)]
        b_h2c = [Buf(f"p3b_h2c{i}") for i in range(2)]
        gT = A.sb("p3b_gT", [128, NFF, 512], BF16)
        b_gT = [Buf(f"p3b_gT{j}") for j in range(NFF)]
        wup = [A.sb(f"p3b_wup{i}", [128, 2, DC, 128], BF16) for i in range(3)]
        b_wup = [Buf(f"p3b_wup{i}") for i in range(3)]
        wdn = [A.sb(f"p3b_wdn{i}", [128, 11, 512], BF16) for i in range(3)]
        b_wdn = [Buf(f"p3b_wdn{i}") for i in range(3)]
        gc = [A.sb(f"p3b_gc{i}", [128, 512], F32) for i in range(2)]
        vc = [A.sb(f"p3b_vc{i}", [128, 512], F32) for i in range(2)]
        sg = [A.sb(f"p3b_sg{i}", [128, 512], F32) for i in range(2)]
        b_gc = [Buf(f"p3b_gc{i}") for i in range(2)]
        b_vc = [Buf(f"p3b_vc{i}") for i in range(2)]
        b_sg = [Buf(f"p3b_sg{i}") for i in range(2)]
        yt = [A.sb(f"p3b_yt{i}", [128, D], F32) for i in range(4)]
        b_yt = [Buf(f"p3b_yt{i}") for i in range(4)]
        junk = A.sb("p3b_junk", [128, D], BF16)
        b_junk = Buf("p3b_junk")
        ssq = A.sb("p3b_ssq", [128, 4], F32)
        rs = A.sb("p3b_rs", [128, 4], F32)
        b_ssq = [Buf(f"p3b_ssq{i}") for i in range(4)]
        b_rs = [Buf(f"p3b_rs{i}") for i in range(4)]
        ugp = [A.ps(f"p3b_ug{i}", [128, 512]) for i in range(2)]
        uvp = [A.ps(f"p3b_uv{i}", [128, 512]) for i in range(2)]
        b_ug = [Buf(f"p3b_ug{i}", excl=True) for i in range(2)]
        b_uv = [Buf(f"p3b_uv{i}", excl=True) for i in range(2)]
        dpp = [A.ps(f"p3b_dp{i}", [128, 512]) for i in range(4)]
        b_dp = [Buf(f"p3b_dp{i}", excl=True) for i in range(4)]
        h2v = K.h2T_s.rearrange("(c p) t -> p c t", p=128)
        wdv = K.wdown_s.rearrange("(j p) n -> p j n", p=128)
        cnt = {"wup": 0, "wdn": 0, "u": 0, "t": 0}

        def load_h2c(ci):
            t0 = CH * ci
            ncols = min(CH, S - t0) + 2
            P.dma("sp", lambda e: e.dma_start(out=h2c[ci % 2][:, :, 0:ncols], in_=h2v[:, :, t0:t0 + ncols]), K.b_h2T, b_h2c[ci % 2])

        wup_q = []

        def load_wup(j):
            wi = cnt["wup"] % 3
            cnt["wup"] += 1
            P.dma("sp", lambda e: e.dma_start(out=wup[wi][:, 0, :, :], in_=K.wup_s[j]), K.b_wup, b_wup[wi])
            P.dma("sp", lambda e: e.dma_start(out=wup[wi][:, 1, :, :], in_=K.wup_s[NFF + j]), K.b_wup, b_wup[wi], partial=True)
            wup_q.append(wi)

        wdn_q = []

        def load_wdn(cg, q):
            wi = cnt["wdn"] % 3
            cnt["wdn"] += 1
            P.dma("sp", lambda e: e.dma_start(out=wdn[wi][:, :, :], in_=wdv[:, 11 * q:11 * (q + 1), 512 * cg:512 * (cg + 1)]), K.b_wdown, b_wdn[wi])
            wdn_q.append(wi)

        load_h2c(0)
        load_wup(0)
        load_wup(1)
        for ci in range(NCH):
            t0 = CH * ci
            nout = min(CH, S - t0)
            ncols = nout + 2
            hs = ci % 2
            if ci + 1 < NCH:
                load_h2c(ci + 1)
            for j in range(NFF):
                nxt = ci * NFF + j + 2
                if nxt < NCH * NFF:
                    load_wup(nxt % NFF)
                wi = wup_q.pop(0)
                ui = cnt["u"] % 2
                cnt["u"] += 1
                _mm_group(P, ugp[ui][:, 0:ncols], b_ug[ui], [(wup[wi][:, 0, c, :], h2c[hs][:, c, 0:ncols]) for c in range(DC)],
                          reads=[b_wup[wi], b_h2c[hs]])
                _mm_group(P, uvp[ui][:, 0:ncols], b_uv[ui], [(wup[wi][:, 1, c, :], h2c[hs][:, c, 0:ncols]) for c in range(DC)],
                          reads=[b_wup[wi], b_h2c[hs]])
                ti = cnt["t"] % 2
                cnt["t"] += 1
                for (up_, b_up, acc, b_acc, tile) in ((ugp[ui], b_ug[ui], gc[ti], b_gc[ti], j), (uvp[ui], b_uv[ui], vc[ti], b_vc[ti], NFF + j)):
                    P.op("act", lambda e, up_=up_, acc=acc, tile=tile, nout=nout: e.activation(
                        out=acc[:, 0:nout], in_=up_[:, 1:nout + 1], func=AF.Identity, scale=cwt[:, tile, 1:2], bias=cbt[:, tile:tile + 1]),
                        reads=[b_up, b_c], writes=[b_acc])
                    for tap, off in ((0, 0), (2, 2)):
                        P.op("dve", lambda e, up_=up_, acc=acc, tile=tile, tap=tap, off=off, nout=nout: e.scalar_tensor_tensor(
                            out=acc[:, 0:nout], in0=up_[:, off:off + nout], scalar=cwt[:, tile, tap:tap + 1], in1=acc[:, 0:nout],
                            op0=ALU.mult, op1=ALU.add),
                            reads=[b_up, b_c, b_acc], writes=[b_acc])
                P.op("act", lambda e, ti=ti, nout=nout: e.activation(out=sg[ti][:, 0:nout], in_=gc[ti][:, 0:nout], func=AF.Silu),
                     reads=[b_gc[ti]], writes=[b_sg[ti]])
                P.op("pool", lambda e, ti=ti, j=j, nout=nout: e.tensor_tensor(out=gT[:, j, 0:nout], in0=sg[ti][:, 0:nout], in1=vc[ti][:, 0:nout], op=ALU.mult),
                     reads=[b_sg[ti], b_vc[ti]], writes=[b_gT[j]])
            ntt = (nout + 127) // 128
            for tt in range(ntt):
                m = min(128, nout - 128 * tt)
                P.dma("sp", lambda e, tt=tt, m=m, t0=t0: e.dma_start(out=yt[tt][0:m, :], in_=K.x1_s[t0 + 128 * tt:t0 + 128 * tt + m, :]), K.b_x1, b_yt[tt])
            load_wdn(0, 0)
            load_wdn(0, 1)
            for cg in range(4):
                for q in range(4):
                    nxt = cg * 4 + q + 2
                    if nxt < 16:
                        load_wdn(nxt // 4, nxt % 4)
                    wi = wdn_q.pop(0)
                    for tt in range(ntt):
                        m = min(128, nout - 128 * tt)
                        for jj in range(11):
                            j = 11 * q + jj
                            first, last = (j == 0), (j == NFF - 1)
                            sig = first or last or jj == 10
                            P.op("pe", lambda e, tt=tt, m=m, j=j, jj=jj, wi=wi, first=first, last=last: e.matmul(
                                dpp[tt][0:m, :], lhsT=gT[:, j, 128 * tt:128 * tt + m], rhs=wdn[wi][:, jj, :], start=first, stop=last),
                                reads=([b_wdn[wi], b_gT[j]] if (jj == 0 or jj == 10) else [b_gT[j]]),
                                writes=[b_dp[tt]] if first else (), pwrites=[b_dp[tt]] if (sig and not first) else (),
                                signal=sig)
                for tt in range(ntt):
                    m = min(128, nout - 128 * tt)
                    P.op("dve", lambda e, tt=tt, m=m, cg=cg: e.tensor_tensor(
                        out=yt[tt][0:m, 512 * cg:512 * (cg + 1)], in0=dpp[tt][0:m, :], in1=yt[tt][0:m, 512 * cg:512 * (cg + 1)], op=ALU.add),
                        reads=[b_dp[tt], b_yt[tt]], writes=[b_yt[tt]])
            for tt in range(ntt):
                m = min(128, nout - 128 * tt)
                P.op("act", lambda e, tt=tt, m=m: e.activation(out=junk[0:m, :], in_=yt[tt][0:m, :], func=AF.Square, accum_out=ssq[0:m, tt:tt + 1]),
                     reads=[b_yt[tt]], writes=[b_junk, b_ssq[tt]])
                P.op("act", lambda e, tt=tt, m=m: e.activation(out=rs[0:m, tt:tt + 1], in_=ssq[0:m, tt:tt + 1], func=AF.Ln, scale=1.0 / D, bias=K.epst[0:m, 0:1]),
                     reads=[b_ssq[tt], K.b_eps], writes=[b_rs[tt]])
                P.op("act", lambda e, tt=tt, m=m: e.activation(out=rs[0:m, tt:tt + 1], in_=rs[0:m, tt:tt + 1], func=AF.Exp, scale=-0.5),
                     reads=[b_rs[tt]], writes=[b_rs[tt]])
                P.op("dve", lambda e, tt=tt, m=m: e.scalar_tensor_tensor(
                    out=yt[tt][0:m, :], in0=yt[tt][0:m, :], scalar=rs[0:m, tt:tt + 1], in1=fgt[0:m, :], op0=ALU.mult, op1=ALU.mult),
                    reads=[b_yt[tt], b_rs[tt], b_c], writes=[b_yt[tt]])
                P.dma("sp", lambda e, tt=tt, m=m, t0=t0: e.dma_start(out=K.y[t0 + 128 * tt:t0 + 128 * tt + m, :], in_=yt[tt][0:m, :]),
                      b_yt[tt], K.b_y, owner="src", partial=True)
    P.barrier()


def _pc(v, n):
    return np.ascontiguousarray(np.asarray(v, np.float32).reshape(n, 128).T)


def make_in_maps(ins, seqs):
    cst = _constants()
    w_in = np.ascontiguousarray(ins["w_in"][0], dtype=np.float32)
    w_out = np.ascontiguousarray(ins["w_out"][0], dtype=np.float32)
    w_up = np.ascontiguousarray(ins["w_up"][0], dtype=np.float32)
    w_down = np.ascontiguousarray(ins["w_down"][0], dtype=np.float32)
    g1 = _pc(ins["norm1_g"][0], DC)
    g2 = _pc(ins["norm2_g"][0], DC)
    gm = _pc(np.concatenate([ins["attn_out_g"][0], ins["fourier_out_g"][0]]), DC)
    fg = np.ascontiguousarray(np.broadcast_to(np.asarray(ins["final_g"], np.float32)[None, :], (128, D)))
    cw = np.ascontiguousarray(np.asarray(ins["conv_w"][0], np.float32).reshape(3, 2 * NFF, 128).transpose(2, 1, 0))
    cb = _pc(ins["conv_b"][0], 2 * NFF)
    shared = dict(w_in=w_in, w_out=w_out, w_up=w_up, w_down=w_down, g1=g1, g2=g2, gm=gm, fg=fg, cw=cw, cb=cb, **cst)
    return [dict(shared, x=np.ascontiguousarray(s, dtype=np.float32)) for s in seqs]


_PROG = {}


def kernel(x_prompt, x_sample, norm1_g, w_in, attn_out_g, fourier_out_g, w_out,
           norm2_g, w_up, conv_w, conv_b, w_down, final_g):
    ins = dict(x_prompt=x_prompt, x_sample=x_sample, norm1_g=norm1_g, w_in=w_in, attn_out_g=attn_out_g,
               fourier_out_g=fourier_out_g, w_out=w_out, norm2_g=norm2_g, w_up=w_up, conv_w=conv_w,
               conv_b=conv_b, w_down=w_down, final_g=final_g)
    ins = {k: np.asarray(v) for k, v in ins.items()}
    seqs = [ins["x_prompt"][i] for i in range(4)] + [ins["x_sample"][i] for i in range(2)]
    idle = np.zeros_like(seqs[0])
    placed = [seqs[0], seqs[1], seqs[2], idle, seqs[3], seqs[4], seqs[5], idle]
    maps = make_in_maps(ins, placed)
    if "nc" not in _PROG:
        _PROG["nc"] = build_program()
    res = run_bass_kernel_spmd(_PROG["nc"], maps, core_ids=list(range(N_CORES)))
    ys = [np.asarray(res.results[i]["y"], dtype=np.float32) for i in (0, 1, 2, 4, 5, 6)]
    return (np.stack(ys[:4], axis=0), np.stack(ys[4:6], axis=0))
```

```python
import math
import numpy as np
import concourse.bass as bass
import concourse.mybir as mybir
from concourse.bass_utils import run_bass_kernel_spmd

F32 = mybir.dt.float32
BF16 = mybir.dt.bfloat16
AF = mybir.ActivationFunctionType
ALU = mybir.AluOpType

S = 8192
D = 2048
DC = D // 128
NH = 24
DH = 64
AW = NH * DH
FW = 512
NQKV = 3 * AW + FW
DFF = 5632
NFF = DFF // 128
EPS = 1e-6
PAD = 1024
SEM_ROLL = 24000
N_CORES = 8


class Buf:
    __slots__ = ("name", "w", "r", "fw", "dsem_w", "dsem_r", "excl")

    def __init__(self, name, excl=False):
        self.name = name
        self.excl = excl
        self.w = {}
        self.r = {}
        self.fw = {}
        self.dsem_w = None
        self.dsem_r = None


class Prog:
    ENG = ("pe", "act", "dve", "pool", "sp")

    def __init__(self, nc):
        self.nc = nc
        self.q = {e: [] for e in self.ENG}
        self.sem_handles = []
        self.cur_sem = {}
        self.seen = {e: {} for e in self.ENG}
        self.n_ops = 0
        self.dsems = []

    def _new_sem(self, name):
        h = self.nc.alloc_semaphore(name)
        self.sem_handles.append(h)
        return len(self.sem_handles) - 1

    def _eng_event(self, eng):
        cur = self.cur_sem.get(eng)
        if cur is None or cur[1] >= SEM_ROLL:
            cur = [self._new_sem(f"pg_{eng}_{len(self.sem_handles)}"), 0]
            self.cur_sem[eng] = cur
        cur[1] += 1
        return (cur[0], cur[1])

    def _collect(self, eng, reads, writes, pwrites):
        deps = {}

        def add(d):
            for s, v in d.items():
                if deps.get(s, 0) < v:
                    deps[s] = v
        for b in reads:
            add(b.w)
            if b.excl:
                add(b.r)
        for b in writes:
            add(b.w)
            add(b.r)
        for b in pwrites:
            add(b.r)
            add(b.fw)
        seen = self.seen[eng]
        waits = []
        for s, v in deps.items():
            if seen.get(s, 0) < v:
                seen[s] = v
                waits.append((s, v))
        return waits

    def _publish(self, ev, reads, writes, pwrites):
        s, v = ev
        for b in reads:
            if b.r.get(s, 0) < v:
                b.r[s] = v
        for b in writes:
            b.w = {s: v}
            b.fw = {s: v}
            b.r = {}
        for b in pwrites:
            if b.w.get(s, 0) < v:
                b.w[s] = v

    def op(self, eng, fn, reads=(), writes=(), pwrites=(), signal=True):
        waits = self._collect(eng, reads, writes, pwrites)
        ev = None
        if signal:
            ev = self._eng_event(eng)
            self._publish(ev, reads, writes, pwrites)
        self.q[eng].append((waits, fn, ev, 1))
        self.n_ops += 1
        return ev

    def dma(self, eng, fn, src, dst, owner="dst", partial=False):
        if owner == "dst":
            if dst.dsem_w is None:
                dst.dsem_w = [self._new_sem(f"dw_{dst.name}"), 0]
                self.dsems.append(dst.dsem_w)
            ds = dst.dsem_w
        else:
            if src.dsem_r is None:
                src.dsem_r = [self._new_sem(f"dr_{src.name}"), 0]
                self.dsems.append(src.dsem_r)
            ds = src.dsem_r
        reads = (src,)
        writes = () if partial else (dst,)
        pwrites = (dst,) if partial else ()
        waits = self._collect(eng, reads, writes, pwrites)
        serial = not (partial and owner == "dst")
        seen = self.seen[eng]
        if serial and ds[1] > 0 and seen.get(ds[0], 0) < ds[1]:
            seen[ds[0]] = ds[1]
            waits.append((ds[0], ds[1]))
        ds[1] += 16
        ev = (ds[0], ds[1])
        self._publish(ev, reads, writes, pwrites)
        self.q[eng].append((waits, fn, ev, 16))
        self.n_ops += 1
        return ev

    def barrier(self):
        evs = [(c[0], c[1]) for c in self.cur_sem.values()] + [(d[0], d[1]) for d in self.dsems if d[1] > 0]
        for eng in self.ENG:
            seen = self.seen[eng]
            waits = []
            for s, v in evs:
                if seen.get(s, 0) < v:
                    seen[s] = v
                    waits.append((s, v))
            if waits:
                self.q[eng].append((waits, None, None, 0))

    def final_wait(self, eng, bufs):
        waits = self._collect(eng, bufs, (), ())
        self.q[eng].append((waits, None, None, 0))

    def emit(self):
        nc = self.nc
        sems = self.sem_handles

        def run(eng_key):
            def body(e):
                for waits, fn, ev, inc in self.q[eng_key]:
                    for s, v in waits:
                        e.wait_ge(sems[s], v)
                    if fn is None:
                        continue
                    ins = fn(e)
                    if ev is not None:
                        ins.then_inc(sems[ev[0]], inc)
            return body

        with nc.Block() as block:
            block.tensor(run("pe"))
            block.scalar(run("act"))
            block.vector(run("dve"))
            block.gpsimd(run("pool"))
            block.sync(run("sp"))


def _rope_tables():
    half = 8
    inv_freq = (np.float32(500000.0) ** (-(np.arange(0, 16, 2, dtype=np.float32) / np.float32(16)))).astype(np.float32)
    ang = (np.arange(S, dtype=np.float32)[:, None] * inv_freq[None, :]).astype(np.float32)
    cos = np.cos(ang.astype(np.float64)).astype(np.float32)
    sin = np.sin(ang.astype(np.float64)).astype(np.float32)
    C = np.ones((128, S), np.float32)
    Sn = np.zeros((128, S), np.float32)
    for p in range(128):
        dim = p % 64
        if dim < 16:
            C[p] = cos[:, dim % half]
            Sn[p] = sin[:, dim % half]
    R = np.zeros((128, 128), np.float32)
    for hh in range(2):
        for j in range(8):
            R[64 * hh + j + 8, 64 * hh + j] = -1.0
            R[64 * hh + j, 64 * hh + j + 8] = 1.0
    return C, Sn, R


def _attn_masks():
    j = np.arange(128)[:, None]
    p = np.arange(128)[None, :]
    u_of_p = p // 32
    i_of_p = p % 32
    A = np.zeros((4, 128, 512), np.float32)
    for w in range(4):
        tq = 4 * i_of_p + u_of_p
        A[w, :, 0:128] = (np.abs(j - 64 - tq) <= 64)
        A[w, :, 128:256] = (np.abs(j + 64 - tq) <= 64)
        ii = np.arange(32)[None, :]
        for u in range(4):
            for k in range(2):
                c0 = 256 + (2 * u + k) * 32
                if k == 0:
                    A[w, :, c0:c0 + 32] = (np.abs(j - 64 - 32 * w - ii) <= 64)
                else:
                    A[w, :, c0:c0 + 32] = (np.abs(j + 64 - 32 * w - ii) <= 64)
    B = np.zeros((4, 128, 256), np.float32)
    los = [0, 0, 8, 16, 24]
    wid = [8, 16, 16, 16, 8]
    offs = [0, 32, 96, 160, 224]
    for r in range(4):
        for m in range(5):
            for u in range(4):
                il = np.arange(wid[m])[None, :]
                ip = los[m] + il
                tokq = 16 * ip + 4 * u + r
                tokk = 128 * m - 64 + j
                c0 = offs[m] + u * wid[m]
                B[r, :, c0:c0 + wid[m]] = (np.abs(tokk - tokq) <= 64)
    return A, B


def _dft_tables():
    c = np.arange(128)
    ang = 2 * np.pi * np.outer(c, c) / 128.0
    CS = np.stack([np.cos(ang), -np.sin(ang)], axis=1).astype(np.float32)
    s1 = np.arange(128)[:, None, None]
    s2 = np.arange(64)[None, :, None]
    k1 = np.arange(128)[None, None, :]
    angG = 2 * np.pi * ((64 * s1 + s2) * k1 % S) / float(S)
    G = np.stack([np.cos(angG), np.sin(angG)], axis=2).astype(np.float32)
    s2v = np.arange(64)[:, None]
    k2 = np.arange(64)[None, :]
    angE = 2 * np.pi * (s2v * k2 % 64) / 64.0
    scale = 1.0 / math.sqrt(S * 128.0)
    E = np.concatenate([np.cos(angE), np.sin(angE)], axis=0).astype(np.float32) * np.float32(scale)
    return CS, G, E


_CONST_CACHE = {}


def _constants():
    if not _CONST_CACHE:
        C, Sn, R = _rope_tables()
        A, B = _attn_masks()
        CS, G, E = _dft_tables()
        _CONST_CACHE.update(dict(
            c_ident=np.eye(128, dtype=np.float32), c_ropeC=C, c_ropeS=Sn, c_rmat=R,
            c_maskA=np.ascontiguousarray(A.transpose(1, 0, 2)),
            c_maskB=np.ascontiguousarray(B.transpose(1, 0, 2)),
            c_dftCS=CS, c_dftG=G, c_dftE=E))
    return _CONST_CACHE


class _Ctx:
    pass


def _mm_group(P, out_ap, b_out, terms, reads, start_first=True):
    n = len(terms)
    for i, (l, r) in enumerate(terms):
        first, last = (i == 0), (i == n - 1)
        P.op("pe",
             lambda e, l=l, r=r, first=first, last=last: e.matmul(out_ap, lhsT=l, rhs=r, start=first, stop=last),
             reads=reads if (first or last) else (),
             writes=[b_out] if first else (),
             pwrites=() if first else ([b_out] if last else ()),
             signal=(first or last))


def build_program(upto="all", debug=False, lim=None):
    nc = bass.Bass("TRN2", target_bir_lowering=False)
    P = Prog(nc)
    K = _Ctx()
    K.nc, K.P, K.debug = nc, P, debug
    K.lim = lim or {}
    dbg_kind = "ExternalOutput" if debug else "Internal"

    def din(name, shape, dt=F32):
        return nc.dram_tensor(name, list(shape), dt, kind="ExternalInput").ap()

    def dscr(name, shape, dt, dbg=False):
        return nc.dram_tensor(name, list(shape), dt, kind=(dbg_kind if dbg else "Internal")).ap()

    K.x = din("x", [S, D])
    K.w_in = din("w_in", [D, NQKV])
    K.w_out = din("w_out", [D, D])
    K.w_up = din("w_up", [D, 2 * DFF])
    K.w_down = din("w_down", [DFF, D])
    K.g1 = din("g1", [128, DC])
    K.gm = din("gm", [128, DC])
    K.g2 = din("g2", [128, DC])
    K.fg = din("fg", [128, D])
    K.cw = din("cw", [128, 2 * NFF, 3])
    K.cb = din("cb", [128, 2 * NFF])
    K.c_ident = din("c_ident", [128, 128])
    K.c_ropeC = din("c_ropeC", [128, S])
    K.c_ropeS = din("c_ropeS", [128, S])
    K.c_rmat = din("c_rmat", [128, 128])
    K.c_maskA = din("c_maskA", [128, 4, 512])
    K.c_maskB = din("c_maskB", [128, 4, 256])
    K.c_dftCS = din("c_dftCS", [128, 2, 128])
    K.c_dftG = din("c_dftG", [128, 64, 2, 128])
    K.c_dftE = din("c_dftE", [128, 64])
    K.y = nc.dram_tensor("y", [S, D], F32, kind="ExternalOutput").ap()

    K.qkvT = dscr("qkvT_s", [NQKV, S], BF16, dbg=True)
    K.FT_s = dscr("FT_s", [FW, S], BF16, dbg=True)
    K.B_s = dscr("B_s", [4, 2, 64, 128, 128], BF16)
    K.attnT_s = dscr("attnT_s", [AW, S], BF16, dbg=True)
    K.x1_s = dscr("x1_s", [S, D], F32, dbg=True)
    K.h2T_s = dscr("h2T_s", [D, S + 2], BF16, dbg=True)
    K.wout_s = dscr("wout_s", [D, D], BF16)
    K.wup_s = dscr("wup_s", [2 * NFF, 128, DC, 128], BF16)
    K.wdown_s = dscr("wdown_s", [DFF, D], BF16)

    K.b_in = Buf("dram_in")
    K.b_y = Buf("y")
    K.b_qkvT = Buf("qkvT_s")
    K.b_FT = Buf("FT_s")
    K.b_Bs = [Buf(f"B_s{g}") for g in range(4)]
    K.b_attnT = Buf("attnT_s")
    K.b_x1 = Buf("x1_s")
    K.b_h2T = Buf("h2T_s")
    K.b_wout = Buf("wout_s")
    K.b_wup = Buf("wup_s")
    K.b_wdown = Buf("wdown_s")

    sb = nc.alloc_sbuf_tensor
    K.ident = sb("ident", [128, 128], BF16)
    K.b_ident = Buf("ident")
    P.dma("pool", lambda e: e.dma_start(out=K.ident[:, :], in_=K.c_ident), K.b_in, K.b_ident)
    K.g1t = sb("g1t", [128, DC], F32)
    K.g2t = sb("g2t", [128, DC], F32)
    K.gmt = sb("gmt", [128, DC], F32)
    K.b_gains = Buf("gains")
    P.dma("sp", lambda e: e.dma_start(out=K.g1t[:, :], in_=K.g1), K.b_in, K.b_gains, partial=True)
    P.dma("sp", lambda e: e.dma_start(out=K.g2t[:, :], in_=K.g2), K.b_in, K.b_gains, partial=True)
    P.dma("sp", lambda e: e.dma_start(out=K.gmt[:, :], in_=K.gm), K.b_in, K.b_gains, partial=True)
    K.epst = sb("epst", [128, 1], F32)
    K.b_eps = Buf("eps")
    P.op("dve", lambda e: e.memset(K.epst[:, :], EPS), writes=[K.b_eps])

    base = nc.sbuf_bytes_remaining
    phases = ["p1", "pf", "p2", "p3a", "p3b"]
    stop = phases.index(upto) if upto in phases else len(phases) - 1
    out_bufs = []
    _phase1(K)
    if stop >= 1:
        _phase_fnet(K)
    if stop >= 2:
        _phase_attn(K)
    if stop >= 3:
        _phase3a(K)
    if stop >= 4:
        _phase3b(K)
    finals = [K.b_y, K.b_qkvT, K.b_FT, K.b_attnT, K.b_x1, K.b_h2T, K.b_wout, K.b_wup, K.b_wdown] + K.b_Bs
    P.final_wait("sp", finals)
    P.emit()
    return nc


class _Arena:
    def __init__(self, K):
        from contextlib import ExitStack
        self.nc = K.nc
        self.st = ExitStack()

    def __enter__(self):
        self.st.__enter__()
        return self

    def __exit__(self, *a):
        return self.st.__exit__(*a)

    def sb(self, name, shape, dt):
        return self.st.enter_context(self.nc.sbuf_tensor(name, list(shape), dt))

    def ps(self, name, shape, dt=F32):
        return self.st.enter_context(self.nc.psum_tensor(name, list(shape), dt))


def _interleave(main, side, ratio):
    side_live = side is not None
    n = 0
    for _ in main:
        n += 1
        if side_live and n % ratio == 0:
            try:
                next(side)
            except StopIteration:
                side_live = False
    if side_live:
        for _ in side:
            pass


def _rstd_chain(P, ssq_ap, b_ssq, rstd_ap, b_rstd, scale, K):
    P.op("act", lambda e: e.activation(out=rstd_ap, in_=ssq_ap, func=AF.Ln, scale=scale, bias=K.epst[:, 0:1]),
         reads=[b_ssq, K.b_eps], writes=[b_rstd])
    P.op("act", lambda e: e.activation(out=rstd_ap, in_=rstd_ap, func=AF.Exp, scale=-0.5),
         reads=[b_rstd], writes=[b_rstd])


def _phase1(K):
    nc, P = K.nc, K.P
    NSC, TSC = K.lim.get("p1_nsc", 8), 1024
    with _Arena(K) as A:
        xt = [A.sb(f"p1_xt{i}", [128, D], F32) for i in range(3)]
        b_xt = [Buf(f"p1_xt{i}") for i in range(3)]
        junk = A.sb("p1_junk", [128, D], BF16)
        b_junk = Buf("p1_junk")
        xb = [A.sb(f"p1_xb{i}", [128, 4, D], BF16) for i in range(2)]
        b_xb = [[Buf(f"p1_xb{i}_{j}") for j in range(4)] for i in range(2)]
        ssq = A.sb("p1_ssq", [128, 64], F32)
        rstd = A.sb("p1_rstd", [128, 64], F32)
        b_ssq = [Buf(f"p1_ssq{t}") for t in range(64)]
        b_rstd = [Buf(f"p1_rstd{t}") for t in range(64)]
        hT = [A.sb(f"p1_hT{i}", [128, DC, TSC], BF16) for i in range(2)]
        b_hT = [[Buf(f"p1_hT{i}_{h}") for h in range(2)] for i in range(2)]
        wsl = [A.sb(f"p1_w{i}", [128, DC, 512], BF16) for i in range(2)]
        b_w = [Buf(f"p1_w{i}") for i in range(2)]
        ropeC = [A.sb(f"p1_rc{i}", [128, TSC], F32) for i in range(2)]
        ropeS = [A.sb(f"p1_rs{i}", [128, TSC], F32) for i in range(2)]
        b_rope = [Buf(f"p1_rope{i}") for i in range(2)]
        rmat = A.sb("p1_rmat", [128, 128], BF16)
        b_rmat = Buf("p1_rmat")
        ob = [A.sb(f"p1_ob{i}", [128, TSC], BF16) for i in range(3)]
        b_ob = [Buf(f"p1_ob{i}") for i in range(3)]
        qb = [A.sb(f"p1_qb{i}", [128, 512], BF16) for i in range(3)]
        b_qb = [Buf(f"p1_qb{i}") for i in range(3)]
        t1 = [A.sb(f"p1_t1{i}", [128, 512], F32) for i in range(3)]
        t2 = [A.sb(f"p1_t2{i}", [128, 512], F32) for i in range(3)]
        b_t1 = [Buf(f"p1_t1{i}") for i in range(3)]
        b_t2 = [Buf(f"p1_t2{i}") for i in range(3)]
        psm = [A.ps(f"p1_psm{i}", [128, 512]) for i in range(3)]
        b_psm = [Buf(f"p1_psm{i}", excl=True) for i in range(3)]
        psr = [A.ps(f"p1_psr{i}", [128, 512]) for i in range(2)]
        b_psr = [Buf(f"p1_psr{i}", excl=True) for i in range(2)]
        pT = [A.ps(f"p1_pT{i}", [128, 8, 128], BF16) for i in range(2)]
        b_pT = [Buf(f"p1_pT{i}", excl=True) for i in range(2)]

        P.dma("pool", lambda e: e.dma_start(out=rmat[:, :], in_=K.c_rmat), K.b_in, b_rmat)

        cnt = {"xt": 0, "pT": 0, "psm": 0, "psr": 0, "ob": 0, "qb": 0, "ev": 0, "w": 0}

        def gen_hT(sc):
            slot = sc % 2
            for h in range(2):
                xs = (2 * sc + h) % 2
                for j in range(4):
                    tt = sc * 8 + h * 4 + j
                    xi = cnt["xt"] % 3
                    cnt["xt"] += 1
                    P.dma("sp", lambda e, xi=xi, tt=tt: e.dma_start(out=xt[xi][:, :], in_=K.x[tt * 128:(tt + 1) * 128, :]),
                          K.b_in, b_xt[xi])
                    P.op("act", lambda e, xi=xi, tt=tt: e.activation(out=junk[:, :], in_=xt[xi][:, :], func=AF.Square,
                                                                     accum_out=ssq[:, tt:tt + 1]),
                         reads=[b_xt[xi]], writes=[b_junk, b_ssq[tt]])
                    _rstd_chain(P, ssq[:, tt:tt + 1], b_ssq[tt], rstd[:, tt:tt + 1], b_rstd[tt], 1.0 / D, K)
                    P.op("dve", lambda e, xi=xi, tt=tt, xs=xs, j=j: e.tensor_scalar(
                        out=xb[xs][:, j, :], in0=xt[xi][:, :], scalar1=rstd[:, tt:tt + 1], scalar2=None, op0=ALU.mult),
                        reads=[b_xt[xi], b_rstd[tt]], writes=[b_xb[xs][j]])
                    yield
                for c in range(DC):
                    pi = cnt["pT"] % 2
                    cnt["pT"] += 1
                    for j in range(4):
                        P.op("pe", lambda e, pi=pi, j=j, c=c, xs=xs: e.transpose(
                            out=pT[pi][:, j, :], in_=xb[xs][:, j, c * 128:(c + 1) * 128], identity=K.ident[:, :]),
                            reads=[b_xb[xs][j], K.b_ident],
                            writes=[b_pT[pi]] if j == 0 else (), pwrites=() if j == 0 else [b_pT[pi]],
                            signal=True)
                    dst = hT[slot][:, c, h * 512:(h + 1) * 512]
                    src = pT[pi][:, 0:4, :]
                    if c % 2 == 0:
                        P.op("act", lambda e, dst=dst, src=src, c=c: e.activation(
                            out=dst.rearrange("p (j t) -> p j t", j=4), in_=src, func=AF.Copy, scale=K.g1t[:, c:c + 1]),
                            reads=[b_pT[pi], K.b_gains], writes=[b_hT[slot][h]] if c == 0 else (),
                            pwrites=() if c == 0 else [b_hT[slot][h]])
                    else:
                        P.op("dve", lambda e, dst=dst, src=src, c=c: e.tensor_scalar(
                            out=dst.rearrange("p (j t) -> p j t", j=4), in0=src, scalar1=K.g1t[:, c:c + 1], scalar2=None,
                            op0=ALU.mult),
                            reads=[b_pT[pi], K.b_gains], writes=(), pwrites=[b_hT[slot][h]])
                    yield

        def load_w(wg):
            wi = wg % 2
            P.dma("pool", lambda e, wi=wi, wg=wg: e.dma_start(
                out=wsl[wi][:, :, :], in_=K.w_in[:, (wg % 10) * 512:(wg % 10 + 1) * 512].rearrange("(c p) n -> p c n", p=128)),
                K.b_in, b_w[wi])

        def gen_proj(sc):
            pending = []
            slot = sc % 2
            rs = sc % 2
            P.dma("sp", lambda e: e.dma_start(out=ropeC[rs][:, :], in_=K.c_ropeC[:, sc * TSC:(sc + 1) * TSC]), K.b_in, b_rope[rs])
            P.dma("sp", lambda e: e.dma_start(out=ropeS[rs][:, :], in_=K.c_ropeS[:, sc * TSC:(sc + 1) * TSC]), K.b_in, b_rope[rs],
                  partial=True)
            for wgl in range(10):
                wg = sc * 10 + wgl
                wi = wg % 2
                if wg + 1 < NSC * 10:
                    load_w(wg + 1)
                for ft in range(4):
                    frow = (wgl * 4 + ft) * 128
                    is_rope = wgl < 6
                    oi = cnt["ob"] % 3
                    cnt["ob"] += 1
                    for h in range(2):
                        mi = cnt["psm"] % 3
                        cnt["psm"] += 1
                        terms = [(wsl[wi][:, c, ft * 128:(ft + 1) * 128], hT[slot][:, c, h * 512:(h + 1) * 512]) for c in range(DC)]
                        _mm_group(P, psm[mi][:, :], b_psm[mi], terms, reads=[b_w[wi], b_hT[slot][h]])
                        while pending:
                            pending.pop(0)()
                        odst = ob[oi][:, h * 512:(h + 1) * 512]
                        wkw = dict(writes=[b_ob[oi]]) if h == 0 else dict(pwrites=[b_ob[oi]])
                        if not is_rope:
                            cnt["ev"] += 1
                            if cnt["ev"] % 2 == 0:
                                P.op("act", lambda e, odst=odst, mi=mi: e.activation(out=odst, in_=psm[mi][:, :], func=AF.Copy),
                                     reads=[b_psm[mi]], **wkw)
                            else:
                                P.op("dve", lambda e, odst=odst, mi=mi: e.tensor_copy(out=odst, in_=psm[mi][:, :]),
                                     reads=[b_psm[mi]], **wkw)
                        else:
                            qi = cnt["qb"] % 3
                            cnt["qb"] += 1
                            ri = cnt["psr"] % 2
                            cnt["psr"] += 1
                            P.op("act", lambda e, qi=qi, mi=mi: e.activation(out=qb[qi][:, :], in_=psm[mi][:, :], func=AF.Copy),
                                 reads=[b_psm[mi]], writes=[b_qb[qi]])
                            P.op("dve", lambda e, qi=qi, mi=mi, h=h: e.tensor_tensor(
                                out=t1[qi][:, :], in0=psm[mi][:, :], in1=ropeC[rs][:, h * 512:(h + 1) * 512], op=ALU.mult),
                                reads=[b_psm[mi], b_rope[rs]], writes=[b_t1[qi]])

                            def tail(qi=qi, ri=ri, h=h, odst=odst, wkw=wkw):
                                P.op("pe", lambda e: e.matmul(psr[ri][:, :], lhsT=rmat[:, :], rhs=qb[qi][:, :], start=True, stop=True),
                                     reads=[b_qb[qi], b_rmat], writes=[b_psr[ri]])
                                P.op("dve", lambda e: e.tensor_tensor(
                                    out=t2[qi][:, :], in0=psr[ri][:, :], in1=ropeS[rs][:, h * 512:(h + 1) * 512], op=ALU.mult),
                                    reads=[b_psr[ri], b_rope[rs]], writes=[b_t2[qi]])
                                P.op("pool", lambda e: e.tensor_tensor(out=odst, in0=t1[qi][:, :], in1=t2[qi][:, :], op=ALU.add),
                                     reads=[b_t1[qi], b_t2[qi]], **wkw)
                            pending.append(tail)
                        yield
                    def store(oi=oi, frow=frow):
                        P.dma("sp", lambda e: e.dma_start(
                            out=K.qkvT[frow:frow + 128, sc * TSC:(sc + 1) * TSC], in_=ob[oi][:, :]),
                            b_ob[oi], K.b_qkvT, owner="src", partial=True)
                    pending.append(store)
            while pending:
                pending.pop(0)()

        load_w(0)
        for _ in gen_hT(0):
            pass
        for sc in range(NSC):
            _interleave(gen_proj(sc), gen_hT(sc + 1) if sc + 1 < NSC else None, 2)
    P.barrier()


def _phase_fnet(K):
    nc, P = K.nc, K.P
    NG = K.lim.get("pf_ng", 4)
    with _Arena(K) as A:
        CS = A.sb("pf_CS", [128, 2, 128], BF16)
        G = A.sb("pf_G", [128, 64, 2, 128], BF16)
        E = A.sb("pf_E", [128, 64], BF16)
        b_tab = Buf("pf_tab")
        P.dma("pool", lambda e: e.dma_start(out=CS[:, :, :], in_=K.c_dftCS), K.b_in, b_tab, partial=True)
        P.dma("pool", lambda e: e.dma_start(out=G[:, :, :, :], in_=K.c_dftG), K.b_in, b_tab, partial=True)
        P.dma("pool", lambda e: e.dma_start(out=E[:, :], in_=K.c_dftE), K.b_in, b_tab, partial=True)
        fT = [A.sb(f"pf_fT{i}", [128, S], BF16) for i in range(2)]
        b_fT = [Buf(f"pf_fT{i}") for i in range(2)]
        Zsb = [A.sb(f"pf_Z{i}", [128, 6, 128], BF16) for i in range(3)]
        b_Z = [Buf(f"pf_Z{i}") for i in range(3)]
        Bsb = [A.sb(f"pf_B{i}", [128, 2, 4, 128], BF16) for i in range(3)]
        b_B = [Buf(f"pf_B{i}") for i in range(3)]
        Bp = [A.sb(f"pf_Bp{i}", [128, 128, 128], BF16) for i in range(2)]
        b_Bp = [Buf(f"pf_Bp{i}") for i in range(2)]
        FTo = [A.sb(f"pf_FTo{i}", [128, S], BF16) for i in range(2)]
        b_FTo = [Buf(f"pf_FTo{i}") for i in range(2)]
        psZ = [A.ps(f"pf_psZ{i}", [128, 4, 128]) for i in range(2)]
        b_psZ = [Buf(f"pf_psZ{i}", excl=True) for i in range(2)]
        psB = [A.ps(f"pf_psB{i}", [128, 4, 128]) for i in range(2)]
        b_psB = [Buf(f"pf_psB{i}", excl=True) for i in range(2)]
        psC = [A.ps(f"pf_psC{i}", [128, 8, 64]) for i in range(2)]
        b_psC = [Buf(f"pf_psC{i}", excl=True) for i in range(2)]
        cnt = {"z": 0, "zs": 0, "b": 0, "bs": 0, "c": 0}

        def stage_a(g):
            fi = g % 2
            P.dma("sp", lambda e: e.dma_start(out=fT[fi][:, :], in_=K.qkvT[3 * AW + 128 * g:3 * AW + 128 * (g + 1), :]),
                  K.b_qkvT, b_fT[fi])
            for sp_ in range(32):
                zi = cnt["z"] % 2
                cnt["z"] += 1
                zs = cnt["zs"] % 3
                cnt["zs"] += 1
                for sl in range(2):
                    s2 = 2 * sp_ + sl
                    lhs = fT[fi][:, s2:S:64]
                    for ri in range(2):
                        first = (sl == 0 and ri == 0)
                        last = (sl == 1 and ri == 1)
                        P.op("pe", lambda e, zi=zi, ri=ri, sl=sl, lhs=lhs: e.matmul(psZ[zi][:, 2 * sl + ri, :], lhsT=lhs, rhs=CS[:, ri, :], start=True, stop=True),
                             reads=[b_fT[fi], b_tab] if (first or last) else (), writes=[b_psZ[zi]] if first else (),
                             pwrites=[b_psZ[zi]] if last else (), signal=(first or last))
                zv = Zsb[zs][:, :, :].rearrange("p (s k) c -> p s k c", s=2)
                pz = psZ[zi][:, :, :].rearrange("p (s r) c -> p s r c", s=2)
                P.op("act", lambda e, zv=zv, pz=pz: e.activation(out=zv[:, :, 0:2, :], in_=pz, func=AF.Copy),
                     reads=[b_psZ[zi]], writes=[b_Z[zs]])
                P.op("dve", lambda e, zv=zv, pz=pz: e.tensor_scalar(out=zv[:, :, 2, :], in0=pz[:, :, 0, :], scalar1=-1.0, scalar2=None, op0=ALU.mult),
                     reads=[b_psZ[zi]], pwrites=[b_Z[zs]])
                bi = cnt["b"] % 2
                cnt["b"] += 1
                mm = []
                for sl in range(2):
                    s2 = 2 * sp_ + sl
                    mm.append((psB[bi][:, 2 * sl + 0, :], G[:, s2, 0, :], zv[:, sl, 0, :], True, False))
                    mm.append((psB[bi][:, 2 * sl + 0, :], G[:, s2, 1, :], zv[:, sl, 1, :], False, True))
                    mm.append((psB[bi][:, 2 * sl + 1, :], G[:, s2, 0, :], zv[:, sl, 1, :], True, False))
                    mm.append((psB[bi][:, 2 * sl + 1, :], G[:, s2, 1, :], zv[:, sl, 2, :], False, True))
                for i, (o, l, r, st, sp2) in enumerate(mm):
                    first, last = (i == 0), (i == len(mm) - 1)
                    P.op("pe", lambda e, o=o, l=l, r=r, st=st, sp2=sp2: e.matmul(o, lhsT=l, rhs=r, start=st, stop=sp2),
                         reads=[b_Z[zs], b_tab] if (first or last) else (), writes=[b_psB[bi]] if first else (),
                         pwrites=[b_psB[bi]] if last else (), signal=(first or last))
                if sp_ % 2 == 0:
                    cnt["bs"] += 1
                bs = cnt["bs"] % 3
                half = sp_ % 2
                wkw = dict(writes=[b_B[bs]]) if half == 0 else dict(pwrites=[b_B[bs]])
                dstB = Bsb[bs][:, :, 2 * half:2 * half + 2, :]
                srcB = psB[bi][:, :, :].rearrange("p (s r) c -> p r s c", s=2)
                if sp_ % 2 == 0:
                    P.op("act", lambda e, dstB=dstB, srcB=srcB: e.activation(out=dstB, in_=srcB, func=AF.Copy), reads=[b_psB[bi]], **wkw)
                else:
                    P.op("dve", lambda e, dstB=dstB, srcB=srcB: e.tensor_copy(out=dstB, in_=srcB), reads=[b_psB[bi]], **wkw)
                if half == 1:
                    s0 = 2 * sp_ - 2
                    for ri in range(2):
                        P.dma("sp", lambda e, bs=bs, s0=s0, g=g, ri=ri: e.dma_start(
                            out=K.B_s[g, ri, s0:s0 + 4].rearrange("s k c -> k s c"), in_=Bsb[bs][:, ri, :, :]),
                            b_B[bs], K.b_Bs[g], owner="src", partial=True)

        def stage_c(g):
            pi = g % 2
            P.dma("sp", lambda e: e.dma_start(out=Bp[pi][:, :, :], in_=K.B_s[g].rearrange("r s k c -> (r s) k c")),
                  K.b_Bs[g], b_Bp[pi])
            fo = FTo[pi][:, :].rearrange("p (k2 k1) -> p k1 k2", k1=128)
            for kb in range(16):
                ci = cnt["c"] % 2
                cnt["c"] += 1
                for kl in range(8):
                    k1 = kb * 8 + kl
                    P.op("pe", lambda e, ci=ci, kl=kl, k1=k1: e.matmul(psC[ci][:, kl, :], lhsT=Bp[pi][:, k1, :], rhs=E[:, :], start=True, stop=True),
                         reads=[b_Bp[pi], b_tab], writes=[b_psC[ci]] if kl == 0 else (), pwrites=() if kl == 0 else [b_psC[ci]],
                         signal=(kl == 0 or kl == 7))
                wkw = dict(writes=[b_FTo[pi]]) if kb == 0 else dict(pwrites=[b_FTo[pi]])
                if kb % 2 == 0:
                    P.op("act", lambda e, ci=ci, kb=kb: e.activation(out=fo[:, kb * 8:(kb + 1) * 8, :], in_=psC[ci][:, :, :], func=AF.Copy),
                         reads=[b_psC[ci]], **wkw)
                else:
                    P.op("dve", lambda e, ci=ci, kb=kb: e.tensor_copy(out=fo[:, kb * 8:(kb + 1) * 8, :], in_=psC[ci][:, :, :]),
                         reads=[b_psC[ci]], **wkw)
            P.dma("sp", lambda e: e.dma_start(out=K.FT_s[128 * g:128 * (g + 1), :], in_=FTo[pi][:, :]),
                  b_FTo[pi], K.b_FT, owner="src", partial=True)

        order = []
        for g in range(NG):
            order.append(("a", g))
            if g >= 1:
                order.append(("c", g - 1))
        order.append(("c", NG - 1))
        for kind, g in order:
            (stage_a if kind == "a" else stage_c)(g)
    P.barrier()


_D1_LO = [0, 0, 8, 16, 24]
_D1_W = [8, 16, 16, 16, 8]
_D1_OFF = [0, 32, 96, 160, 224]


def _precast_blocks():
    blocks = []
    for c in range(DC):
        blocks.append(("out", c, 0, D))
    for c in range(DC):
        for q in range(4):
            blocks.append(("up", c, q * 2816, 2816))
    for c in range(NFF):
        blocks.append(("down", c, 0, D))
    return blocks


def _phase_attn(K):
    nc, P = K.nc, K.P
    NHP = K.lim.get("p2_nhp", 12)
    NQT = K.lim.get("p2_nq", 64)
    W = S + 2 * PAD
    with _Arena(K) as A:
        qT = [A.sb(f"p2_qT{i}", [128, S], BF16) for i in range(2)]
        b_qT = [Buf(f"p2_qT{i}") for i in range(2)]
        kT = [A.sb(f"p2_kT{i}", [128, W], BF16) for i in range(2)]
        b_kT = [Buf(f"p2_kT{i}") for i in range(2)]
        vT = A.sb("p2_vT", [128, W], BF16)
        b_vT = Buf("p2_vT")
        V1 = A.sb("p2_V1", [128, 65, 2, 65], BF16)
        V4 = A.sb("p2_V4", [128, 68, 2, 65], BF16)
        V16 = A.sb("p2_V16", [128, 80, 2, 65], BF16)
        b_V = Buf("p2_V")
        mA = A.sb("p2_mA", [128, 4, 512], BF16)
        mB = A.sb("p2_mB", [128, 4, 256], BF16)
        b_mask = Buf("p2_mask")
        PT = [A.sb(f"p2_PT{i}", [128, 768], BF16) for i in range(2)]
        b_PT = [Buf(f"p2_PT{i}") for i in range(2)]
        PTm = [A.sb(f"p2_PTm{i}", [128, 512], BF16) for i in range(4)]
        Pd1 = [A.sb(f"p2_Pd1{i}", [128, 5, 128], BF16) for i in range(4)]
        b_PTm = [Buf(f"p2_PTm{i}") for i in range(4)]
        rec = [A.sb(f"p2_rec{i}", [128, 2], F32) for i in range(2)]
        b_rec = [Buf(f"p2_rec{i}") for i in range(2)]
        apair = [A.sb(f"p2_ap{i}", [128, 2, 64], BF16) for i in range(2)]
        b_apair = [Buf(f"p2_ap{i}") for i in range(2)]
        aT = A.sb("p2_aT", [128, S], BF16)
        b_aT = Buf("p2_aT")
        stg = [A.sb(f"p2_stg{i}", [128, 2816], BF16) for i in range(3)]
        b_stg = [Buf(f"p2_stg{i}") for i in range(3)]
        Sps = [A.ps(f"p2_S{i}", [128, 1024]) for i in range(2)]
        b_S = [Buf(f"p2_S{i}", excl=True) for i in range(2)]
        Ops = [A.ps(f"p2_O{i}", [128, 512]) for i in range(2)]
        b_O = [Buf(f"p2_O{i}", excl=True) for i in range(2)]
        Tps = [A.ps(f"p2_T{i}", [128, 8, 128], BF16) for i in range(2)]
        b_T = [Buf(f"p2_T{i}", excl=True) for i in range(2)]

        P.dma("pool", lambda e: e.dma_start(out=mA[:, :, :], in_=K.c_maskA), K.b_in, b_mask, partial=True)
        P.dma("pool", lambda e: e.dma_start(out=mB[:, :, :], in_=K.c_maskB), K.b_in, b_mask, partial=True)
        for i in range(2):
            P.op("pool", lambda e, i=i: e.memset(kT[i][:, 0:PAD], 0.0), writes=[b_kT[i]])
            P.op("pool", lambda e, i=i: e.memset(kT[i][:, PAD + S:W], 0.0), pwrites=[b_kT[i]])
        P.op("pool", lambda e: e.memset(vT[:, 0:PAD], 0.0), writes=[b_vT])
        P.op("pool", lambda e: e.memset(vT[:, PAD + S:W], 0.0), pwrites=[b_vT])
        for i in range(4):
            P.op("pool", lambda e, i=i: e.memset(Pd1[i][:, :, :], 0.0), writes=[b_PTm[i]])
        if NQT < 64:
            P.op("pool", lambda e: e.memset(aT[:, :], 0.0), writes=[b_aT])
        first = True
        for (Vt, ncls, nt) in ((V1, 1, 65), (V4, 4, 17), (V16, 16, 5)):
            P.op("dve", lambda e, Vt=Vt: e.memset(Vt[:, :, :, :], 0.0), writes=[b_V] if first else (), pwrites=() if first else [b_V])
            first = False
        for (Vt, ncls, nt) in ((V1, 1, 65), (V4, 4, 17), (V16, 16, 5)):
            Vv = Vt[:, :, :, :].rearrange("p (c n) h d -> p c n h d", n=nt)
            P.op("dve", lambda e, Vv=Vv, nt=nt: e.memset(Vv[:, :, 1:nt - 1, :, 64:65], 1.0), pwrites=[b_V], reads=[b_V])
            P.op("dve", lambda e, Vv=Vv: e.memset(Vv[64:128, :, 0:1, :, 64:65], 1.0), pwrites=[b_V], reads=[b_V])
            P.op("dve", lambda e, Vv=Vv, nt=nt: e.memset(Vv[0:64, :, nt - 1:nt, :, 64:65], 1.0), pwrites=[b_V], reads=[b_V])

        blocks = _precast_blocks() if K.lim.get("p2_precast", True) else []
        per_hp = (len(blocks) + NHP - 1) // NHP
        cnt = {"stg": 0, "T": 0, "ev": 0, "blk": 0}

        def precast_one():
            if cnt["blk"] >= len(blocks):
                return
            kind, c, c0, ncol = blocks[cnt["blk"]]
            cnt["blk"] += 1
            si = cnt["stg"] % 3
            cnt["stg"] += 1
            src = {"out": K.w_out, "up": K.w_up, "down": K.w_down}[kind]
            dst = {"out": K.wout_s, "up": K.wup_s, "down": K.wdown_s}[kind]
            bdst = {"out": K.b_wout, "up": K.b_wup, "down": K.b_wdown}[kind]
            P.dma("pool", lambda e: e.dma_start(out=stg[si][:, 0:ncol], in_=src[c * 128:(c + 1) * 128, c0:c0 + ncol]), K.b_in, b_stg[si])
            if kind == "up":
                t0 = c0 // 128
                P.dma("sp", lambda e: e.dma_start(out=K.wup_s[t0:t0 + 22, :, c, :].rearrange("t p n -> p t n"),
                                                  in_=stg[si][:, 0:ncol].rearrange("p (t n) -> p t n", n=128)),
                      b_stg[si], bdst, owner="src", partial=True)
            else:
                P.dma("sp", lambda e: e.dma_start(out=dst[c * 128:(c + 1) * 128, c0:c0 + ncol], in_=stg[si][:, 0:ncol]), b_stg[si], bdst, owner="src", partial=True)

        def load_qk(hp):
            i = hp % 2
            P.dma("sp", lambda e: e.dma_start(out=qT[i][:, :], in_=K.qkvT[128 * hp:128 * (hp + 1), :]), K.b_qkvT, b_qT[i])
            P.dma("sp", lambda e: e.dma_start(out=kT[i][:, PAD:PAD + S], in_=K.qkvT[AW + 128 * hp:AW + 128 * (hp + 1), :]), K.b_qkvT, b_kT[i])

        def load_v(hp):
            P.dma("sp", lambda e: e.dma_start(out=vT[:, PAD:PAD + S], in_=K.qkvT[2 * AW + 128 * hp:2 * AW + 128 * (hp + 1), :]), K.b_qkvT, b_vT)

        def kcols(pat, cls, n):
            if pat == 1:
                return PAD - 64 + 128 * n, 1
            if pat == 4:
                return PAD - 256 + 512 * n + cls, 4
            return PAD - 1024 + 2048 * n + cls, 16

        def v_arrange():
            jobs = []
            for n in range(65):
                jobs.append((V1, n, kcols(1, 0, n)))
            for r in range(4):
                for n in range(17):
                    jobs.append((V4, r * 17 + n, kcols(4, r, n)))
            for c in range(16):
                for n in range(5):
                    jobs.append((V16, c * 5 + n, kcols(16, c, n)))
            i = 0
            while i < len(jobs):
                Vt, t0, _ = jobs[i]
                grp = [jobs[i]]
                while len(grp) < 8 and i + len(grp) < len(jobs) and jobs[i + len(grp)][0] is Vt:
                    grp.append(jobs[i + len(grp)])
                ti = cnt["T"] % 2
                cnt["T"] += 1
                for gi, (_, tidx, (c0, st)) in enumerate(grp):
                    src = vT[:, c0:c0 + 127 * st + 1:st]
                    P.op("pe", lambda e, ti=ti, gi=gi, src=src: e.transpose(out=Tps[ti][:, gi, :], in_=src, identity=K.ident[:, :]),
                         reads=[b_vT, K.b_ident], writes=[b_T[ti]] if gi == 0 else (), pwrites=() if gi == 0 else [b_T[ti]],
                         signal=(gi == 0 or gi == len(grp) - 1))
                ng = len(grp)
                dst = Vt[:, t0:t0 + ng, :, 0:64]
                srcp = Tps[ti][:, 0:ng, :].rearrange("p g (h d) -> p g h d", h=2)
                cnt["ev"] += 1
                if cnt["ev"] % 2 == 0:
                    P.op("act", lambda e, dst=dst, srcp=srcp: e.activation(out=dst, in_=srcp, func=AF.Copy), reads=[b_T[ti]], pwrites=[b_V])
                else:
                    P.op("dve", lambda e, dst=dst, srcp=srcp: e.tensor_copy(out=dst, in_=srcp), reads=[b_T[ti]], pwrites=[b_V])
                i += ng

        def qcols(qs, a, r, hh, lo=0, hi=32):
            v = qT[qs][64 * hh:64 * hh + 64, 512 * a:512 * a + 512].rearrange("p (i u r) -> p r u i", u=4, r=4)
            return v[:, r, :, lo:hi]

        def score_mms(hp, qi, hh):
            a, r = qi // 4, qi % 4
            w = a % 4
            ks = hp % 2
            Sb = Sps[hh]
            kk = kT[ks]
            rows = slice(64 * hh, 64 * hh + 64)
            mms = []
            for t in range(2):
                c0, st = kcols(4, r, a + t)
                mms.append((Sb[:, t * 128:(t + 1) * 128], kk[rows, c0:c0 + 127 * st + 1:st], qcols(ks, a, r, hh)))
            n16 = a // 4
            for u in range(4):
                c16 = 4 * u + r
                qv = qT[ks][rows, 512 * a + c16:512 * a + c16 + 31 * 16 + 1:16]
                for k in range(2):
                    c0, st = kcols(16, c16, n16 + k)
                    o0 = 256 + (2 * u + k) * 32
                    mms.append((Sb[:, o0:o0 + 32], kk[rows, c0:c0 + 127 * st + 1:st], qv))
            for m in range(5):
                c0, st = kcols(1, 0, 4 * a + m)
                o0 = 512 + _D1_OFF[m]
                wd = _D1_W[m]
                mms.append((Sb[:, o0:o0 + 4 * wd].rearrange("p (u i) -> p u i", u=4), kk[rows, c0:c0 + 128],
                            qcols(ks, a, r, hh, _D1_LO[m], _D1_LO[m] + wd)))
            return mms

        def score_post(hp, qi, hh):
            a, r = qi // 4, qi % 4
            w = a % 4
            Sb = Sps[hh]
            P.op("act", lambda e: e.activation(out=PT[hh][:, :], in_=Sb[:, 0:768], func=AF.Exp, scale=0.125),
                 reads=[b_S[hh]], writes=[b_PT[hh]])
            pm = 2 * (qi % 2) + hh
            eng = "dve" if hh == 0 else "pool"
            P.op(eng, lambda e: e.tensor_tensor(out=PTm[pm][:, :], in0=PT[hh][:, 0:512], in1=mA[:, w, :], op=ALU.mult),
                 reads=[b_PT[hh], b_mask], writes=[b_PTm[pm]])
            src = PT[hh][:, 512 + 32:512 + 224].rearrange("p (m u i) -> p m u i", m=3, u=4)
            msk = mB[:, r, 32:224].rearrange("p (m u i) -> p m u i", m=3, u=4)
            P.op(eng, lambda e: e.tensor_tensor(out=_pd1_mid(Pd1[pm]), in0=src, in1=msk, op=ALU.mult),
                 reads=[b_PT[hh], b_mask], pwrites=[b_PTm[pm]])
            P.op(eng, lambda e: e.tensor_tensor(out=_pd1_edge(Pd1[pm]), in0=_pt_edge(PT[hh]), in1=_mb_edge(mB, r), op=ALU.mult),
                 reads=[b_PT[hh], b_mask], pwrites=[b_PTm[pm]])

        def score(hp, qi):
            ks = hp % 2
            m0 = score_mms(hp, qi, 0)
            m1 = score_mms(hp, qi, 1)
            n = len(m0)
            for i in range(n):
                for hh, mm in ((0, m0), (1, m1)):
                    o, l, rr = mm[i]
                    P.op("pe", lambda e, o=o, l=l, rr=rr: e.matmul(o, lhsT=l, rhs=rr, start=True, stop=True),
                         reads=[b_kT[ks], b_qT[ks]] if (i == 0 or i == n - 1) else (),
                         writes=[b_S[hh]] if i == 0 else (), pwrites=[b_S[hh]] if i == n - 1 else (),
                         signal=(i == 0 or i == n - 1))
            for hh in range(2):
                score_post(hp, qi, hh)

        def pv(hp, qi):
            a, r = qi // 4, qi % 4
            oi = qi % 2
            for hh in range(2):
                pm = 2 * (qi % 2) + hh
                o = Ops[oi][:, 65 * hh:65 * hh + 65]
                mms = []
                for t in range(2):
                    mms.append((o, PTm[pm][:, t * 128:(t + 1) * 128], V4[:, r * 17 + a + t, hh, :]))
                n16 = a // 4
                for u in range(4):
                    c16 = 4 * u + r
                    for k in range(2):
                        o0 = 256 + (2 * u + k) * 32
                        mms.append((Ops[oi][32 * u:32 * u + 32, 65 * hh:65 * hh + 65], PTm[pm][:, o0:o0 + 32], V16[:, c16 * 5 + n16 + k, hh, :]))
                for m in range(5):
                    mms.append((o, Pd1[pm][:, m, :], V1[:, 4 * a + m, hh, :]))
                n = len(mms)
                for i, (oo, l, rr) in enumerate(mms):
                    tp = (0, 32 * ((i - 2) // 2)) if 2 <= i < 10 else None
                    P.op("pe", lambda e, oo=oo, l=l, rr=rr, i=i, n=n, tp=tp: e.matmul(oo, lhsT=l, rhs=rr, start=(i == 0), stop=(i == n - 1), tile_position=tp),
                         reads=[b_PTm[pm], b_V] if (i == 0 or i == n - 1) else (),
                         writes=[b_O[oi]] if (i == 0 and hh == 0) else (),
                         pwrites=[b_O[oi]] if (i == n - 1 or (i == 0 and hh == 1)) else (),
                         signal=(i == 0 or i == n - 1))
            ov = Ops[oi][:, 0:130].rearrange("p (h d) -> p h d", h=2)
            P.op("dve", lambda e: e.reciprocal(out=rec[oi][:, :], in_=ov[:, :, 64]), reads=[b_O[oi]], writes=[b_rec[oi]])
            P.op("dve", lambda e: e.tensor_scalar(out=apair[oi][:, 0, :], in0=ov[:, 0, 0:64], scalar1=rec[oi][:, 0:1], scalar2=None, op0=ALU.mult),
                 reads=[b_O[oi], b_rec[oi]], writes=[b_apair[oi]])
            P.op("act", lambda e: e.activation(out=apair[oi][:, 1, :], in_=ov[:, 1, 0:64], func=AF.Copy, scale=rec[oi][:, 1:2]),
                 reads=[b_O[oi], b_rec[oi]], pwrites=[b_apair[oi]])

        def tr(hp, qi):
            a, r = qi // 4, qi % 4
            oi = qi % 2
            ti = cnt["T"] % 2
            cnt["T"] += 1
            P.op("pe", lambda e: e.transpose(out=Tps[ti][:, 0, :], in_=apair[oi][:, :, :].rearrange("p h d -> p (h d)"), identity=K.ident[:, :]),
                 reads=[b_apair[oi], K.b_ident], writes=[b_T[ti]])
            dst = aT[:, 512 * a:512 * a + 512].rearrange("p (i u r) -> p r u i", u=4, r=4)[:, r, :, :]
            srcp = Tps[ti][:, 0, :].rearrange("p (u i) -> p u i", u=4)
            wkw = dict(writes=[b_aT]) if (qi == 0 and NQT == 64) else dict(pwrites=[b_aT])
            cnt["ev"] += 1
            if cnt["ev"] % 2 == 0:
                P.op("act", lambda e: e.activation(out=dst, in_=srcp, func=AF.Copy), reads=[b_T[ti]], **wkw)
            else:
                P.op("dve", lambda e: e.tensor_copy(out=dst, in_=srcp), reads=[b_T[ti]], **wkw)

        load_qk(0)
        load_v(0)
        for hp in range(NHP):
            v_arrange()
            if hp + 1 < NHP:
                load_qk(hp + 1)
                load_v(hp + 1)
            score(hp, 0)
            nblk = 0
            for qi in range(NQT):
                if qi + 1 < NQT:
                    score(hp, qi + 1)
                pv(hp, qi)
                if qi >= 1:
                    tr(hp, qi - 1)
                if (qi * per_hp) // NQT >= nblk and nblk < per_hp:
                    precast_one()
                    nblk += 1
            tr(hp, NQT - 1)
            P.dma("sp", lambda e, hp=hp: e.dma_start(out=K.attnT_s[128 * hp:128 * (hp + 1), :], in_=aT[:, :]), b_aT, K.b_attnT,
                  owner="src", partial=True)
        while cnt["blk"] < len(blocks):
            precast_one()
    P.barrier()


def _cap(base_ap, dims):
    return bass.AP(base_ap.tensor, base_ap.offset, [list(base_ap.ap[0])] + [list(d) for d in dims])


def _pd1_mid(t):
    return _cap(t[:, 1, 0:1], [(136, 3), (32, 4), (1, 16)])


def _pd1_edge(t):
    return _cap(t[:, 0, 0:1], [(4 * 128 + 24, 2), (32, 4), (1, 8)])


def _pt_edge(pt):
    return _cap(pt[:, 512:513], [(224, 2), (8, 4), (1, 8)])


def _mb_edge(mB, r):
    return _cap(mB[:, r, 0:1], [(224, 2), (8, 4), (1, 8)])


def _phase3a(K):
    nc, P = K.nc, K.P
    NG = K.lim.get("p3a_ng", 16)
    with _Arena(K) as A:
        wout = A.sb("p3a_wout", [128, DC, D], BF16)
        b_wo = Buf("p3a_wout")
        for q in range(4):
            P.dma("sp", lambda e, q=q: e.dma_start(out=wout[:, 4 * q:4 * q + 4, :],
                                                   in_=K.wout_s[512 * q:512 * (q + 1), :].rearrange("(c p) n -> p c n", p=128)),
                  K.b_wout, b_wo, partial=True)
        for c in range(DC):
            eng = "dve" if c % 2 == 0 else "act"
            if eng == "dve":
                P.op("dve", lambda e, c=c: e.tensor_scalar(out=wout[:, c, :], in0=wout[:, c, :], scalar1=K.gmt[:, c:c + 1], scalar2=None, op0=ALU.mult),
                     reads=[b_wo, K.b_gains], pwrites=[b_wo])
            else:
                P.op("act", lambda e, c=c: e.activation(out=wout[:, c, :], in_=wout[:, c, :], func=AF.Copy, scale=K.gmt[:, c:c + 1]),
                     reads=[b_wo, K.b_gains], pwrites=[b_wo])
        aF = [A.sb(f"p3a_aF{i}", [128, DC, 512], BF16) for i in range(2)]
        b_aF = [Buf(f"p3a_aF{i}") for i in range(2)]
        sq = A.sb("p3a_sq", [128, DC, 512], BF16)
        b_sq = Buf("p3a_sq")
        ones = A.sb("p3a_ones", [128, 2], BF16)
        b_ones = Buf("p3a_ones")
        P.op("dve", lambda e: e.memset(ones[:, 0:1], 1.0 / AW), writes=[b_ones])
        P.op("dve", lambda e: e.memset(ones[:, 1:2], 1.0 / FW), pwrites=[b_ones])
        xt = [A.sb(f"p3a_xt{i}", [128, D], F32) for i in range(2)]
        b_xt = [Buf(f"p3a_xt{i}") for i in range(2)]
        x1t = [A.sb(f"p3a_x1t{i}", [128, D], F32) for i in range(2)]
        b_x1t = [Buf(f"p3a_x1t{i}") for i in range(2)]
        junk = A.sb("p3a_junk", [128, D], BF16)
        b_junk = Buf("p3a_junk")
        xb2 = A.sb("p3a_xb2", [128, 4, D], BF16)
        b_xb2 = [Buf(f"p3a_xb2_{j}") for j in range(4)]
        h2st = [A.sb(f"p3a_h2st{i}", [128, DC, 512], BF16) for i in range(2)]
        b_h2st = [Buf(f"p3a_h2st{i}") for i in range(2)]
        rs_af = A.sb("p3a_rsaf", [128, 4, 2], F32)
        b_rsaf = Buf("p3a_rsaf")
        ssq2 = A.sb("p3a_ssq2", [128, 4], F32)
        rs2 = A.sb("p3a_rs2", [128, 4], F32)
        b_ssq2 = [Buf(f"p3a_ssq2_{j}") for j in range(4)]
        b_rs2 = [Buf(f"p3a_rs2_{j}") for j in range(4)]
        zt = A.sb("p3a_zt", [128, DC, 1], BF16)
        b_zt = Buf("p3a_zt")
        Aps = [A.ps(f"p3a_A{i}", [128, 512]) for i in range(2)]
        b_A = [Buf(f"p3a_A{i}", excl=True) for i in range(2)]
        Fps = [A.ps(f"p3a_F{i}", [128, 512]) for i in range(2)]
        b_F = [Buf(f"p3a_F{i}", excl=True) for i in range(2)]
        Tps = [A.ps(f"p3a_T{i}", [128, 8, 128], BF16) for i in range(2)]
        b_T = [Buf(f"p3a_T{i}", excl=True) for i in range(2)]
        ssp = A.ps("p3a_ssp", [128, 4, 2])
        b_ssp = Buf("p3a_ssp", excl=True)
        cnt = {"x": 0, "af": 0, "T": 0, "ev": 0}

        P.op("dve", lambda e: e.memset(zt[:, :, :], 0.0), writes=[b_zt])
        h2v = K.h2T_s.rearrange("(c p) t -> p c t", p=128)
        P.dma("sp", lambda e: e.dma_start(out=h2v[:, :, 0:1], in_=zt[:, :, :], allow_slow_non_contiguous=True), b_zt, K.b_h2T, owner="src", partial=True)
        P.dma("sp", lambda e: e.dma_start(out=h2v[:, :, S + 1:S + 2], in_=zt[:, :, :], allow_slow_non_contiguous=True), b_zt, K.b_h2T, owner="src", partial=True)

        def load_aF(gi):
            si = gi % 2
            P.dma("sp", lambda e: e.dma_start(out=aF[si][:, 0:12, :],
                                              in_=K.attnT_s[:, 512 * gi:512 * (gi + 1)].rearrange("(c p) t -> p c t", p=128)),
                  K.b_attnT, b_aF[si])
            P.dma("sp", lambda e: e.dma_start(out=aF[si][:, 12:16, :],
                                              in_=K.FT_s[:, 512 * gi:512 * (gi + 1)].rearrange("(c p) t -> p c t", p=128)),
                  K.b_FT, b_aF[si], partial=True)

        rs_af2 = [rs_af, A.sb("p3a_rsaf1", [128, 4, 2], F32)]
        b_rsaf2 = [b_rsaf, Buf("p3a_rsaf1")]

        def stats(gi):
            si = gi % 2
            P.op("act", lambda e: e.activation(out=sq[:, :, :], in_=aF[si][:, :, :], func=AF.Square),
                 reads=[b_aF[si]], writes=[b_sq])
            for j in range(4):
                for br, (c0, c1) in enumerate(((0, 12), (12, 16))):
                    for c in range(c0, c1):
                        first, last = (c == c0), (c == c1 - 1)
                        P.op("pe", lambda e, j=j, c=c, br=br, first=first, last=last: e.matmul(
                            ssp[:, j, br:br + 1], lhsT=sq[:, c, 128 * j:128 * (j + 1)], rhs=ones[:, br:br + 1], start=first, stop=last),
                            reads=[b_sq, b_ones] if (first or last) else (),
                            writes=[b_ssp] if (first and j == 0 and br == 0) else (),
                            pwrites=[b_ssp] if (last or (first and not (j == 0 and br == 0))) else (),
                            signal=(first or last))
            _rstd_chain(P, ssp[:, :, :], b_ssp, rs_af2[si][:, :, :], b_rsaf2[si], 1.0, K)

        def load_x(tt):
            xi = tt % 2
            P.dma("sp", lambda e: e.dma_start(out=xt[xi][:, :], in_=K.x[128 * tt:128 * (tt + 1), :]), K.b_in, b_xt[xi])

        load_aF(0)
        stats(0)
        load_x(0)
        for gi in range(NG):
            si = gi % 2
            if gi + 1 < NG:
                load_aF(gi + 1)
            for j in range(4):
                tt = 4 * gi + j
                xi = tt % 2
                if tt + 1 < 4 * NG:
                    load_x(tt + 1)
                for cg in range(4):
                    ai = cnt["af"] % 2
                    cnt["af"] += 1
                    _mm_group(P, Aps[ai][:, :], b_A[ai],
                              [(aF[si][:, c, 128 * j:128 * (j + 1)], wout[:, c, 512 * cg:512 * (cg + 1)]) for c in range(12)],
                              reads=[b_aF[si], b_wo])
                    _mm_group(P, Fps[ai][:, :], b_F[ai],
                              [(aF[si][:, c, 128 * j:128 * (j + 1)], wout[:, c, 512 * cg:512 * (cg + 1)]) for c in range(12, 16)],
                              reads=[b_aF[si], b_wo])
                    if j == 2 and cg == 0 and gi + 1 < NG:
                        stats(gi + 1)
                    dst = x1t[xi][:, 512 * cg:512 * (cg + 1)]
                    P.op("dve", lambda e, ai=ai, xi=xi, j=j, cg=cg, dst=dst, si=si: e.scalar_tensor_tensor(
                        out=dst, in0=Aps[ai][:, :], scalar=rs_af2[si][:, j, 0:1], in1=xt[xi][:, 512 * cg:512 * (cg + 1)], op0=ALU.mult, op1=ALU.add),
                        reads=[b_A[ai], b_rsaf2[si], b_xt[xi]], **(dict(writes=[b_x1t[xi]]) if cg == 0 else dict(pwrites=[b_x1t[xi]])))
                    P.op("dve", lambda e, ai=ai, j=j, dst=dst, si=si: e.scalar_tensor_tensor(
                        out=dst, in0=Fps[ai][:, :], scalar=rs_af2[si][:, j, 1:2], in1=dst, op0=ALU.mult, op1=ALU.add),
                        reads=[b_F[ai], b_rsaf2[si], b_x1t[xi]], pwrites=[b_x1t[xi]])
                P.op("act", lambda e, xi=xi, j=j: e.activation(out=junk[:, :], in_=x1t[xi][:, :], func=AF.Square, accum_out=ssq2[:, j:j + 1]),
                     reads=[b_x1t[xi]], writes=[b_junk, b_ssq2[j]])
                _rstd_chain(P, ssq2[:, j:j + 1], b_ssq2[j], rs2[:, j:j + 1], b_rs2[j], 1.0 / D, K)
                P.op("dve", lambda e, xi=xi, j=j: e.tensor_scalar(out=xb2[:, j, :], in0=x1t[xi][:, :], scalar1=rs2[:, j:j + 1], scalar2=None, op0=ALU.mult),
                     reads=[b_x1t[xi], b_rs2[j]], writes=[b_xb2[j]])
                P.dma("sp", lambda e, xi=xi, tt=tt: e.dma_start(out=K.x1_s[128 * tt:128 * (tt + 1), :], in_=x1t[xi][:, :]),
                      b_x1t[xi], K.b_x1, owner="src", partial=True)
            hi = gi % 2
            for c in range(DC):
                ti = cnt["T"] % 2
                cnt["T"] += 1
                for j in range(4):
                    P.op("pe", lambda e, ti=ti, j=j, c=c: e.transpose(out=Tps[ti][:, j, :], in_=xb2[:, j, 128 * c:128 * (c + 1)], identity=K.ident[:, :]),
                         reads=[b_xb2[j], K.b_ident], writes=[b_T[ti]] if j == 0 else (), pwrites=() if j == 0 else [b_T[ti]])
                dst = h2st[hi][:, c, :].rearrange("p (j t) -> p j t", j=4)
                wkw = dict(writes=[b_h2st[hi]]) if c == 0 else dict(pwrites=[b_h2st[hi]])
                if c % 2 == 0:
                    P.op("act", lambda e, ti=ti, c=c, dst=dst: e.activation(out=dst, in_=Tps[ti][:, 0:4, :], func=AF.Copy, scale=K.g2t[:, c:c + 1]),
                         reads=[b_T[ti], K.b_gains], **wkw)
                else:
                    P.op("dve", lambda e, ti=ti, c=c, dst=dst: e.tensor_scalar(out=dst, in0=Tps[ti][:, 0:4, :], scalar1=K.g2t[:, c:c + 1], scalar2=None, op0=ALU.mult),
                         reads=[b_T[ti], K.b_gains], **wkw)
            P.dma("sp", lambda e, hi=hi, gi=gi: e.dma_start(out=h2v[:, :, 1 + 512 * gi:1 + 512 * (gi + 1)], in_=h2st[hi][:, :, :]),
                  b_h2st[hi], K.b_h2T, owner="src", partial=True)
    P.barrier()


def _phase3b(K):
    nc, P = K.nc, K.P
    CH = 510
    nchunks_all = (S + CH - 1) // CH
    NCH = K.lim.get("p3b_nch", nchunks_all)
    with _Arena(K) as A:
        fgt = A.sb("p3b_fg", [128, D], F32)
        cwt = A.sb("p3b_cw", [128, 2 * NFF, 3], F32)
        cbt = A.sb("p3b_cb", [128, 2 * NFF], F32)
        b_c = Buf("p3b_consts")
        P.dma("sp", lambda e: e.dma_start(out=fgt[:, :], in_=K.fg), K.b_in, b_c, partial=True)
        P.dma("sp", lambda e: e.dma_start(out=cwt[:, :, :], in_=K.cw), K.b_in, b_c, partial=True)
        P.dma("sp", lambda e: e.dma_start(out=cbt[:, :], in_=K.cb), K.b_in, b_c, partial=True)
        h2c = [A.sb(f"p3b_h2c{i}", [128, DC, 512], BF16) for i in range(2)]
        b_h2c = [Buf(f"p3b_h2c{i}") for i in range(2)]
        gT = A.sb("p3b_gT", [128, NFF, 512], BF16)
        b_gT = [Buf(f"p3b_gT{j}") for j in range(NFF)]
        wup = [A.sb(f"p3b_wup{i}", [128, 2, DC, 128], BF16) for i in range(3)]
        b_wup = [Buf(f"p3b_wup{i}") for i in range(3)]
        wdn = [A.sb(f"p3b_wdn{i}", [128, 11, 512], BF16) for i in range(3)]
        b_wdn = [Buf(f"p3b_wdn{i}") for i in range(3)]
        gc = [A.sb(f"p3b_gc{i}", [128, 512], F32) for i in range(2)]
        vc = [A.sb(f"p3b_vc{i}", [128, 512], F32) for i in range(2)]
        sg = [A.sb(f"p3b_sg{i}", [128, 512], F32) for i in range(2)]
        b_gc = [Buf(f"p3b_gc{i}") for i in range(2)]
        b_vc = [Buf(f"p3b_vc{i}") for i in range(2)]
        b_sg = [Buf(f"p3b_sg{i}") for i in range(2)]
        yt = [A.sb(f"p3b_yt{i}", [128, D], F32) for i in range(4)]
        b_yt = [Buf(f"p3b_yt{i}") for i in range(4)]
        junk = A.sb("p3b_junk", [128, D], BF16)
        b_junk = Buf("p3b_junk")
        ssq = A.sb("p3b_ssq", [128, 4], F32)
        rs = A.sb("p3b_rs", [128, 4], F32)
        b_ssq = [Buf(f"p3b_ssq{i}") for i in range(4)]
        b_rs = [Buf(f"p3b_rs{i}") for i in range(4)]
        ugp = [A.ps(f"p3b_ug{i}", [128, 512]) for i in range(2)]
        uvp = [A.ps(f"p3b_uv{i}", [128, 512]) for i in range(2)]
        b_ug = [Buf(f"p3b_ug{i}", excl=True) for i in range(2)]
        b_uv = [Buf(f"p3b_uv{i}", excl=True) for i in range(2)]
        dpp = [A.ps(f"p3b_dp{i}", [128, 512]) for i in range(4)]
        b_dp = [Buf(f"p3b_dp{i}", excl=True) for i in range(4)]
        h2v = K.h2T_s.rearrange("(c p) t -> p c t", p=128)
        wdv = K.wdown_s.rearrange("(j p) n -> p j n", p=128)
        cnt = {"wup": 0, "wdn": 0, "u": 0, "t": 0}

        def load_h2c(ci):
            t0 = CH * ci
            ncols = min(CH, S - t0) + 2
            P.dma("sp", lambda e: e.dma_start(out=h2c[ci % 2][:, :, 0:ncols], in_=h2v[:, :, t0:t0 + ncols]), K.b_h2T, b_h2c[ci % 2])

        wup_q = []

        def load_wup(j):
            wi = cnt["wup"] % 3
            cnt["wup"] += 1
            P.dma("sp", lambda e: e.dma_start(out=wup[wi][:, 0, :, :], in_=K.wup_s[j]), K.b_wup, b_wup[wi])
            P.dma("sp", lambda e: e.dma_start(out=wup[wi][:, 1, :, :], in_=K.wup_s[NFF + j]), K.b_wup, b_wup[wi], partial=True)
            wup_q.append(wi)

        wdn_q = []

        def load_wdn(cg, q):
            wi = cnt["wdn"] % 3
            cnt["wdn"] += 1
            P.dma("sp", lambda e: e.dma_start(out=wdn[wi][:, :, :], in_=wdv[:, 11 * q:11 * (q + 1), 512 * cg:512 * (cg + 1)]), K.b_wdown, b_wdn[wi])
            wdn_q.append(wi)

        load_h2c(0)
        load_wup(0)
        load_wup(1)
        for ci in range(NCH):
            t0 = CH * ci
            nout = min(CH, S - t0)
            ncols = nout + 2
            hs = ci % 2
            if ci + 1 < NCH:
                load_h2c(ci + 1)
            for j in range(NFF):
                nxt = ci * NFF + j + 2
                if nxt < NCH * NFF:
                    load_wup(nxt % NFF)
                wi = wup_q.pop(0)
                ui = cnt["u"] % 2
                cnt["u"] += 1
                _mm_group(P, ugp[ui][:, 0:ncols], b_ug[ui], [(wup[wi][:, 0, c, :], h2c[hs][:, c, 0:ncols]) for c in range(DC)],
                          reads=[b_wup[wi], b_h2c[hs]])
                _mm_group(P, uvp[ui][:, 0:ncols], b_uv[ui], [(wup[wi][:, 1, c, :], h2c[hs][:, c, 0:ncols]) for c in range(DC)],
                          reads=[b_wup[wi], b_h2c[hs]])
                ti = cnt["t"] % 2
                cnt["t"] += 1
                for (up_, b_up, acc, b_acc, tile) in ((ugp[ui], b_ug[ui], gc[ti], b_gc[ti], j), (uvp[ui], b_uv[ui], vc[ti], b_vc[ti], NFF + j)):
                    P.op("act", lambda e, up_=up_, acc=acc, tile=tile, nout=nout: e.activation(
                        out=acc[:, 0:nout], in_=up_[:, 1:nout + 1], func=AF.Identity, scale=cwt[:, tile, 1:2], bias=cbt[:, tile:tile + 1]),
                        reads=[b_up, b_c], writes=[b_acc])
                    for tap, off in ((0, 0), (2, 2)):
                        P.op("dve", lambda e, up_=up_, acc=acc, tile=tile, tap=tap, off=off, nout=nout: e.scalar_tensor_tensor(
                            out=acc[:, 0:nout], in0=up_[:, off:off + nout], scalar=cwt[:, tile, tap:tap + 1], in1=acc[:, 0:nout],
                            op0=ALU.mult, op1=ALU.add),
                            reads=[b_up, b_c, b_acc], writes=[b_acc])
                P.op("act", lambda e, ti=ti, nout=nout: e.activation(out=sg[ti][:, 0:nout], in_=gc[ti][:, 0:nout], func=AF.Silu),
                     reads=[b_gc[ti]], writes=[b_sg[ti]])
                P.op("pool", lambda e, ti=ti, j=j, nout=nout: e.tensor_tensor(out=gT[:, j, 0:nout], in0=sg[ti][:, 0:nout], in1=vc[ti][:, 0:nout], op=ALU.mult),
                     reads=[b_sg[ti], b_vc[ti]], writes=[b_gT[j]])
            ntt = (nout + 127) // 128
            for tt in range(ntt):
                m = min(128, nout - 128 * tt)
                P.dma("sp", lambda e, tt=tt, m=m, t0=t0: e.dma_start(out=yt[tt][0:m, :], in_=K.x1_s[t0 + 128 * tt:t0 + 128 * tt + m, :]), K.b_x1, b_yt[tt])
            load_wdn(0, 0)
            load_wdn(0, 1)
            for cg in range(4):
                for q in range(4):
                    nxt = cg * 4 + q + 2
                    if nxt < 16:
                        load_wdn(nxt // 4, nxt % 4)
                    wi = wdn_q.pop(0)
                    for tt in range(ntt):
                        m = min(128, nout - 128 * tt)
                        for jj in range(11):
                            j = 11 * q + jj
                            first, last = (j == 0), (j == NFF - 1)
                            sig = first or last or jj == 10
                            P.op("pe", lambda e, tt=tt, m=m, j=j, jj=jj, wi=wi, first=first, last=last: e.matmul(
                                dpp[tt][0:m, :], lhsT=gT[:, j, 128 * tt:128 * tt + m], rhs=wdn[wi][:, jj, :], start=first, stop=last),
                                reads=([b_wdn[wi], b_gT[j]] if (jj == 0 or jj == 10) else [b_gT[j]]),
                                writes=[b_dp[tt]] if first else (), pwrites=[b_dp[tt]] if (sig and not first) else (),
                                signal=sig)
                for tt in range(ntt):
                    m = min(128, nout - 128 * tt)
                    P.op("dve", lambda e, tt=tt, m=m, cg=cg: e.tensor_tensor(
                        out=yt[tt][0:m, 512 * cg:512 * (cg + 1)], in0=dpp[tt][0:m, :], in1=yt[tt][0:m, 512 * cg:512 * (cg + 1)], op=ALU.add),
                        reads=[b_dp[tt], b_yt[tt]], writes=[b_yt[tt]])
            for tt in range(ntt):
                m = min(128, nout - 128 * tt)
                P.op("act", lambda e, tt=tt, m=m: e.activation(out=junk[0:m, :], in_=yt[tt][0:m, :], func=AF.Square, accum_out=ssq[0:m, tt:tt + 1]),
                     reads=[b_yt[tt]], writes=[b_junk, b_ssq[tt]])
                P.op("act", lambda e, tt=tt, m=m: e.activation(out=rs[0:m, tt:tt + 1], in_=ssq[0:m, tt:tt + 1], func=AF.Ln, scale=1.0 / D, bias=K.epst[0:m, 0:1]),
                     reads=[b_ssq[tt], K.b_eps], writes=[b_rs[tt]])
                P.op("act", lambda e, tt=tt, m=m: e.activation(out=rs[0:m, tt:tt + 1], in_=rs[0:m, tt:tt + 1], func=AF.Exp, scale=-0.5),
                     reads=[b_rs[tt]], writes=[b_rs[tt]])
                P.op("dve", lambda e, tt=tt, m=m: e.scalar_tensor_tensor(
                    out=yt[tt][0:m, :], in0=yt[tt][0:m, :], scalar=rs[0:m, tt:tt + 1], in1=fgt[0:m, :], op0=ALU.mult, op1=ALU.mult),
                    reads=[b_yt[tt], b_rs[tt], b_c], writes=[b_yt[tt]])
                P.dma("sp", lambda e, tt=tt, m=m, t0=t0: e.dma_start(out=K.y[t0 + 128 * tt:t0 + 128 * tt + m, :], in_=yt[tt][0:m, :]),
                      b_yt[tt], K.b_y, owner="src", partial=True)
    P.barrier()


def _pc(v, n):
    return np.ascontiguousarray(np.asarray(v, np.float32).reshape(n, 128).T)


def make_in_maps(ins, seqs):
    cst = _constants()
    w_in = np.ascontiguousarray(ins["w_in"][0], dtype=np.float32)
    w_out = np.ascontiguousarray(ins["w_out"][0], dtype=np.float32)
    w_up = np.ascontiguousarray(ins["w_up"][0], dtype=np.float32)
    w_down = np.ascontiguousarray(ins["w_down"][0], dtype=np.float32)
    g1 = _pc(ins["norm1_g"][0], DC)
    g2 = _pc(ins["norm2_g"][0], DC)
    gm = _pc(np.concatenate([ins["attn_out_g"][0], ins["fourier_out_g"][0]]), DC)
    fg = np.ascontiguousarray(np.broadcast_to(np.asarray(ins["final_g"], np.float32)[None, :], (128, D)))
    cw = np.ascontiguousarray(np.asarray(ins["conv_w"][0], np.float32).reshape(3, 2 * NFF, 128).transpose(2, 1, 0))
    cb = _pc(ins["conv_b"][0], 2 * NFF)
    shared = dict(w_in=w_in, w_out=w_out, w_up=w_up, w_down=w_down, g1=g1, g2=g2, gm=gm, fg=fg, cw=cw, cb=cb, **cst)
    return [dict(shared, x=np.ascontiguousarray(s, dtype=np.float32)) for s in seqs]


_PROG = {}


def kernel(x_prompt, x_sample, norm1_g, w_in, attn_out_g, fourier_out_g, w_out,
           norm2_g, w_up, conv_w, conv_b, w_down, final_g):
    ins = dict(x_prompt=x_prompt, x_sample=x_sample, norm1_g=norm1_g, w_in=w_in, attn_out_g=attn_out_g,
               fourier_out_g=fourier_out_g, w_out=w_out, norm2_g=norm2_g, w_up=w_up, conv_w=conv_w,
               conv_b=conv_b, w_down=w_down, final_g=final_g)
    ins = {k: np.asarray(v) for k, v in ins.items()}
    seqs = [ins["x_prompt"][i] for i in range(4)] + [ins["x_sample"][i] for i in range(2)]
    idle = np.zeros_like(seqs[0])
    placed = [seqs[0], seqs[1], seqs[2], idle, seqs[3], seqs[4], seqs[5], idle]
    maps = make_in_maps(ins, placed)
    if "nc" not in _PROG:
        _PROG["nc"] = build_program()
    res = run_bass_kernel_spmd(_PROG["nc"], maps, core_ids=list(range(N_CORES)))
    ys = [np.asarray(res.results[i]["y"], dtype=np.float32) for i in (0, 1, 2, 4, 5, 6)]
    return (np.stack(ys[:4], axis=0), np.stack(ys[4:6], axis=0))
```
